# Optimizing a Trainium2 kernel written in Bass

```python
import math
import jax, jax.numpy as jnp
from jax import lax
import numpy as np

D_MODEL = 1024
BATCH = 4
SEQ = 4096
DEPTH = 4

N_MIXERS = 3
N_CONV_LAYERS = (DEPTH + 2) // 3
N_POOL_LAYERS = (DEPTH + 1) // 3
N_ATTN_LAYERS = DEPTH // 3
CONV_WIDTH = 3
POOL_WINDOWS = (2, 4, 8, 16)
N_POOL_GROUPS = len(POOL_WINDOWS)
POOL_GROUP_CH = D_MODEL // N_POOL_GROUPS
HEAD_DIM = 128
N_HEADS = D_MODEL // (2 * HEAD_DIM)
Q_BLOCK = 128
NUM_BUCKETS = 32
MAX_DISTANCE = 128
D_FF = ((8 * D_MODEL // 3 + 255) // 256) * 256
EPS = 1e-6

kernel_name = "hybrid_conv_pool_diffattn_encoder"


def _rms(x, g):
    xf = x.astype(jnp.float32)
    y = xf * lax.rsqrt(jnp.mean(xf * xf, axis=-1, keepdims=True) + EPS)
    return (y * g.astype(jnp.float32)).astype(x.dtype)


def _lambda_init(layer_idx):
    return 0.8 - 0.6 * math.exp(-0.3 * layer_idx)


def _t5_bucket(rel):
    half = NUM_BUCKETS // 2
    ret = jnp.where(rel > 0, half, 0)
    n = jnp.abs(rel)
    max_exact = half // 2
    nf = jnp.maximum(n, 1).astype(jnp.float32)
    large = max_exact + (jnp.log(nf / max_exact) / math.log(MAX_DISTANCE / max_exact)
                         * (half - max_exact)).astype(jnp.int32)
    large = jnp.minimum(large, half - 1)
    return ret + jnp.where(n < max_exact, n, large)


def _conv_mixer(xn, w_in, w_conv, w_out):
    h = xn @ w_in
    gate_b, gate_c, hx = jnp.split(h, 3, axis=-1)
    u = gate_c * hx
    up = jnp.pad(u, ((0, 0), (1, 1), (0, 0)))
    conv = w_conv[0] * up[:, :-2] + w_conv[1] * up[:, 1:-1] + w_conv[2] * up[:, 2:]
    return (gate_b * conv) @ w_out


def _pool_mixer(xn, w_pool, scale):
    bsz, s, _ = xn.shape
    xf = xn.astype(jnp.float32)
    cs = jnp.concatenate([jnp.zeros((bsz, 1, D_MODEL), jnp.float32),
                          jnp.cumsum(xf, axis=1)], axis=1)
    t = jnp.arange(s)
    outs = []
    for g, w in enumerate(POOL_WINDOWS):
        lo = jnp.maximum(t - w // 2, 0)
        hi = jnp.minimum(t + w - 1 - w // 2, s - 1)
        sl = slice(g * POOL_GROUP_CH, (g + 1) * POOL_GROUP_CH)
        csg = cs[..., sl]
        win_sum = jnp.take(csg, hi + 1, axis=1) - jnp.take(csg, lo, axis=1)
        cnt = (hi - lo + 1).astype(jnp.float32)
        outs.append(win_sum / cnt[None, :, None] - xf[..., sl])
    p = jnp.stack(outs, axis=2)
    y = jnp.einsum('bsgc,gcd->bsgd', p, w_pool.astype(jnp.float32)).reshape(bsz, s, D_MODEL)
    return (y * scale.astype(jnp.float32)).astype(xn.dtype)


def _diff_attention(xn, w_qkv, w_o, lq1, lk1, lq2, lk2, subln_g, rel_bias, lambda_init):
    bsz, s, _ = xn.shape
    qkv = xn @ w_qkv
    q, k, v = jnp.split(qkv, 3, axis=-1)
    q = q.reshape(bsz, s, N_HEADS, 2, HEAD_DIM) * (HEAD_DIM ** -0.5)
    k = k.reshape(bsz, s, N_HEADS, 2, HEAD_DIM)
    v = v.reshape(bsz, s, N_HEADS, 2 * HEAD_DIM).transpose(0, 2, 1, 3)
    lam = (jnp.exp(jnp.sum(lq1.astype(jnp.float32) * lk1.astype(jnp.float32)))
           - jnp.exp(jnp.sum(lq2.astype(jnp.float32) * lk2.astype(jnp.float32)))
           + lambda_init)
    nb = s // Q_BLOCK
    qb = q.reshape(bsz, nb, Q_BLOCK, N_HEADS, 2, HEAD_DIM).transpose(1, 0, 3, 4, 2, 5)
    kt = k.transpose(0, 2, 3, 1, 4)
    kpos = jnp.arange(s)

    def block(args):
        qblk, start = args
        qpos = start + jnp.arange(Q_BLOCK)
        bucket = _t5_bucket(kpos[None, :] - qpos[:, None])
        bias = jnp.take(rel_bias, bucket, axis=0).transpose(2, 0, 1).astype(jnp.float32)
        logits = jnp.einsum('bhtqd,bhtkd->bhtqk', qblk, kt).astype(jnp.float32) + bias[None, :, None]
        p = jax.nn.softmax(logits, axis=-1)
        a = p[:, :, 0] - lam * p[:, :, 1]
        return jnp.einsum('bhqk,bhke->bhqe', a.astype(v.dtype), v)

    starts = jnp.arange(nb) * Q_BLOCK
    o = lax.map(block, (qb, starts))
    o = o.transpose(1, 0, 3, 2, 4).reshape(bsz, s, N_HEADS, 2 * HEAD_DIM)
    o = _rms(o, subln_g) * (1.0 - lambda_init)
    return o.reshape(bsz, s, D_MODEL) @ w_o


def _swiglu(xn, w_gate, w_up, w_down):
    return (jax.nn.silu(xn @ w_gate) * (xn @ w_up)) @ w_down


def setup_inputs(seed: int = 0) -> dict:
    key = jax.random.key(seed)
    ks = jax.random.split(key, 18)
    n = jax.random.normal
    f32 = jnp.float32
    sd = D_MODEL ** -0.5
    return {
        "x": n(ks[0], (BATCH, SEQ, D_MODEL), f32),
        "norm_g": 1.0 + 0.1 * n(ks[1], (DEPTH, 4, D_MODEL), f32),
        "conv_w_in": sd * n(ks[2], (N_CONV_LAYERS, D_MODEL, 3 * D_MODEL), f32),
        "conv_w": (CONV_WIDTH ** -0.5) * n(ks[3], (N_CONV_LAYERS, CONV_WIDTH, D_MODEL), f32),
        "conv_w_out": sd * n(ks[4], (N_CONV_LAYERS, D_MODEL, D_MODEL), f32),
        "pool_w": (POOL_GROUP_CH ** -0.5) * n(ks[5], (N_POOL_LAYERS, N_POOL_GROUPS, POOL_GROUP_CH, POOL_GROUP_CH), f32),
        "pool_scale": 1.0 + 0.1 * n(ks[6], (N_POOL_LAYERS, D_MODEL), f32),
        "attn_w_qkv": sd * n(ks[7], (N_ATTN_LAYERS, D_MODEL, 3 * D_MODEL), f32),
        "attn_w_o": sd * n(ks[8], (N_ATTN_LAYERS, D_MODEL, D_MODEL), f32),
        "lambda_q1": 0.1 * n(ks[9], (N_ATTN_LAYERS, HEAD_DIM), f32),
        "lambda_k1": 0.1 * n(ks[10], (N_ATTN_LAYERS, HEAD_DIM), f32),
        "lambda_q2": 0.1 * n(ks[11], (N_ATTN_LAYERS, HEAD_DIM), f32),
        "lambda_k2": 0.1 * n(ks[12], (N_ATTN_LAYERS, HEAD_DIM), f32),
        "attn_subln_g": 1.0 + 0.1 * n(ks[13], (N_ATTN_LAYERS, 2 * HEAD_DIM), f32),
        "rel_bias": 0.5 * n(ks[14], (NUM_BUCKETS, N_HEADS), f32),
        "ffn_w_gate": sd * n(ks[15], (DEPTH, D_MODEL, D_FF), f32),
        "ffn_w_up": sd * n(ks[16], (DEPTH, D_MODEL, D_FF), f32),
        "ffn_w_down": (D_FF ** -0.5) * n(ks[17], (DEPTH, D_FF, D_MODEL), f32),
    }


def reference(x, norm_g, conv_w_in, conv_w, conv_w_out, pool_w, pool_scale,
              attn_w_qkv, attn_w_o, lambda_q1, lambda_k1, lambda_q2, lambda_k2,
              attn_subln_g, rel_bias, ffn_w_gate, ffn_w_up, ffn_w_down):
    ia, ib, ic = 0, 0, 0
    for i in range(DEPTH):
        g = norm_g[i]
        hn = _rms(x, g[0])
        kind = i % N_MIXERS
        if kind == 0:
            m = _conv_mixer(hn, conv_w_in[ia], conv_w[ia], conv_w_out[ia])
            ia += 1
        elif kind == 1:
            m = _pool_mixer(hn, pool_w[ib], pool_scale[ib])
            ib += 1
        else:
            m = _diff_attention(hn, attn_w_qkv[ic], attn_w_o[ic], lambda_q1[ic], lambda_k1[ic],
                                lambda_q2[ic], lambda_k2[ic], attn_subln_g[ic], rel_bias,
                                _lambda_init(i))
            ic += 1
        x = x + _rms(m, g[1])
        f = _swiglu(_rms(x, g[2]), ffn_w_gate[i], ffn_w_up[i], ffn_w_down[i])
        x = x + _rms(f, g[3])
    return x
```

```python
import math
from contextlib import ExitStack
import numpy as np
import concourse.bass as bass
import concourse.mybir as mybir
from concourse.bass_utils import run_bass_kernel_spmd

F32 = mybir.dt.float32
BF16 = mybir.dt.bfloat16
AF = mybir.ActivationFunctionType
ALU = mybir.AluOpType

D = 1024
SEQ = 4096
T = 2048
NB = 8
CH = 512
NCH = 4
DFF = 2816
FB = 22
EPS = 1e-6
DEPTH = 4
PAIRS = [[0, 1], [2, 3], [4, 5], [6, 7]]
POOLW = (2, 4, 8, 16)

C_G = 0
C_CW = 128
C_PS = 176
C_SG = 184
C_ML = 186
C_MR = 187
C_CL = 188
C_CR = 220
C_BC = 252
C_EPS = 264
NCONST = 266


def lambda_init(i):
    return 0.8 - 0.6 * math.exp(-0.3 * i)


class Ev:
    __slots__ = ("sem", "val")

    def __init__(self, sem, val):
        self.sem = sem
        self.val = val


class Res:
    def __init__(self, name):
        self.name = name
        self.w = None
        self.rs = {}
        self.dsem = None
        self.dcnt = 0


ENGS = ("pe", "act", "dve", "pool", "sp")


class Sched:
    def __init__(self, nc, es):
        self.nc = nc
        self.es = es
        self.q = {e: [] for e in ENGS}
        self.cnt = {e: 0 for e in ENGS}
        self.esem = {e: es.enter_context(nc.semaphore("es_" + e)) for e in ENGS}
        self.seen = {e: {} for e in ENGS}
        self.nsem = 0

    def res(self, name):
        return Res(name)

    def op(self, eng, fn, reads=(), writes=(), dma=False, inc=True):
        waits = {}

        def need(ev):
            if ev is not None:
                k = id(ev.sem)
                if k not in waits or waits[k].val < ev.val:
                    waits[k] = ev

        own = self.esem[eng]
        for r in reads:
            need(r.w)
        for w in writes:
            if not (eng == "pe" and w.w is not None and w.w.sem is own):
                need(w.w)
            for e in w.rs.values():
                need(e)
        seen = self.seen[eng]
        ws = []
        for k, ev in waits.items():
            if seen.get(k, 0) < ev.val:
                seen[k] = ev.val
                ws.append((ev.sem, ev.val))
        if dma:
            dst = writes[0]
            if dst.dsem is None:
                dst.dsem = self.es.enter_context(self.nc.semaphore("d%d_%s" % (self.nsem, dst.name)))
                self.nsem += 1
            dst.dcnt += 16
            ev = Ev(dst.dsem, dst.dcnt)
            amt = 16
        else:
            self.cnt[eng] += 1
            ev = Ev(own, self.cnt[eng])
            amt = 1
        for r in reads:
            r.rs[id(ev.sem)] = ev
        for w in writes:
            w.w = ev
            w.rs = {}
        self.q[eng].append((ws, fn, ev.sem, amt))
        return ev

    def final_wait(self, eng, evs):
        ws = [(e.sem, e.val) for e in evs]
        self.q[eng].append((ws, None, None, 0))

    def emit(self):
        nc = self.nc
        q = self.q

        def run(e, lst):
            for ws, fn, sem, amt in lst:
                for s, v in ws:
                    e.wait_ge(s, v)
                if fn is not None:
                    fn(e).then_inc(sem, amt)

        with nc.Block() as block:
            @block.tensor
            def _(e):
                run(e, q["pe"])

            @block.scalar
            def _(e):
                run(e, q["act"])

            @block.vector
            def _(e):
                run(e, q["dve"])

            @block.gpsimd
            def _(e):
                run(e, q["pool"])

            @block.sync
            def _(e):
                run(e, q["sp"])


STAGE = 99
SUB = 99


def build(layers, has_attn=True):
    nc = bass.Bass("TRN2", target_bir_lowering=False)
    es = ExitStack()
    S = Sched(nc, es)
    dt = nc.dram_tensor

    class Lazy:
        def __init__(self, name, shape):
            self.name, self.shape, self.h = name, shape, None

        def ap(self):
            if self.h is None:
                self.h = dt(self.name, self.shape, F32, kind="ExternalInput").ap()
                used_inputs.append(self.name)
            return self.h

        def __getitem__(self, idx):
            return self.ap()[idx]

    used_inputs = []
    nc._used_inputs = used_inputs
    x_in = Lazy("x_in", [128, NB * T]).ap()
    y_out = dt("y_out", [128, NB * T], F32, kind="ExternalOutput").ap()
    consts_d = Lazy("consts", [128, NCONST]).ap()
    wg_d = [Lazy("wg%d" % l, [FB, 128, NB * 128]) for l in range(DEPTH)]
    wu_d = [Lazy("wu%d" % l, [FB, 128, NB * 128]) for l in range(DEPTH)]
    wd_d = [Lazy("wd%d" % l, [NB, 128, FB * 128]) for l in range(DEPTH)]
    cwin_d = Lazy("cwin", [2, 24, 128, NB * 128])
    cwout_d = Lazy("cwout", [2, NB, 128, NB * 128])
    pw_d = Lazy("pw", [128, 4 * 2 * 256])
    aqk_d = Lazy("aqk", [16, 128, NB * 128])
    av_d = Lazy("av", [128, NB * 1024])
    ao_d = Lazy("ao", [NB, 128, NB * 128])
    lamv_d = Lazy("lamv", [128, 512])
    strips_d = Lazy("strips", [4, 128, 2 * 1152])

    edge_in = dt("edge_in", [128, 16], F32)
    edge_out = dt("edge_out", [2 * 128, 16], F32)
    pedge_in = dt("pedge_in", [128, 128], F32)
    pedge_out = dt("pedge_out", [2 * 128, 128], F32)
    kt_in = [dt("kt_in%d" % i, [4 * 128, T], BF16) for i in range(2)]
    kt_all = [dt("kt_all%d" % i, [2 * 4 * 128, T], BF16) for i in range(2)]
    v_in = [dt("v_in%d" % i, [8 * 128, D], BF16) for i in range(2)]
    v_all = [dt("v_all%d" % i, [2 * 8 * 128, D], BF16) for i in range(2)]

    sb = lambda name, shape, dtype: es.enter_context(nc.sbuf_tensor(name, shape, dtype))
    X = sb("X", [128, NB * T], F32)
    HN = sb("HN", [128, NB * T], BF16)
    BIG = sb("BIG", [128, FB * 1024], BF16)
    M = sb("M", [128, NB * 1024], F32)
    W8 = sb("W8", [128, 6 * 1024], BF16)
    W22 = sb("W22", [128, 2 * FB * 128], BF16)
    RSTD = sb("RSTD", [128, 2 * CH], F32)
    SQ = sb("SQ", [128, 2 * CH], BF16)
    TMP = sb("TMP", [128, 2 * CH], F32)
    CONST = sb("CONST", [128, NCONST], F32)
    ONES = sb("ONES", [128, 128], BF16)
    SMALL = sb("SMALL", [128, 160], F32)
    PS = [es.enter_context(nc.psum_tensor("ps%d" % i, [128, CH], F32)) for i in range(8)]

    r_X = [S.res("X%d" % c) for c in range(NCH)]
    r_HN = [S.res("HN%d" % c) for c in range(NCH)]
    r_PS = [S.res("ps%d" % i) for i in range(8)]
    r_W8 = [S.res("w8_%d" % i) for i in range(6)]
    r_W22 = [S.res("w22_%d" % i) for i in range(2)]
    r_RSTD = [S.res("rstd%d" % i) for i in range(2)]
    r_SQ = [S.res("sq%d" % i) for i in range(2)]
    r_TMP = [S.res("tmp%d" % i) for i in range(2)]
    r_M = [S.res("M%d" % i) for i in range(2)]
    r_CONST = S.res("const")
    r_ONES = S.res("ones")
    r_SMALL = S.res("small")
    r_BIG = S.res("big")
    r_edge_in = S.res("edge_in")
    r_edge_out = S.res("edge_out")

    def xs(k, c, n=CH):
        o = k * T + c * CH
        return X[:, o:o + n]

    def hs(k, c):
        o = k * T + c * CH
        return HN[:, o:o + CH]

    def w8s(slot, k):
        o = slot * 1024 + k * 128
        return W8[:, o:o + 128]

    def w22s(slot, f):
        o = slot * FB * 128 + f * 128
        return W22[:, o:o + 128]

    def cc(col, n=1):
        return CONST[:, col:col + n]

    rot = {"sq": 0, "tmp": 0, "rstd": 0, "w8": 0, "w22": 0}

    def nxt(name, n):
        v = rot[name]
        rot[name] = (v + 1) % n
        return v

    S.op("sp", lambda e: e.dma_start(out=CONST[:], in_=consts_d[:, :]), writes=[r_CONST], dma=True)
    for c in range(NCH):
        for k in range(NB):
            pass
    for c in range(NCH):
        S.op("sp", lambda e, c=c: e.dma_start(
            out=X[:].rearrange("p (k t) -> p k t", k=NB)[:, :, c * CH:(c + 1) * CH],
            in_=x_in.rearrange("p (k t) -> p k t", k=NB)[:, :, c * CH:(c + 1) * CH]),
            writes=[r_X[c]], dma=True)
    S.op("dve", lambda e: e.memset(ONES[:], 1.0), writes=[r_ONES])
    S.op("dve", lambda e: e.tensor_scalar(out=cc(C_G, 128), in0=cc(C_G, 128), scalar1=32.0, scalar2=None,
                                         op0=ALU.mult), reads=[r_CONST], writes=[r_CONST])

    def gcol(l, j, k):
        return cc(C_G + (l * 4 + j) * 8 + k)

    def rstd_from_ps(ps_i, r, dim_eps, out_ap=None, out_res=None):
        o = RSTD[:, r * CH:(r + 1) * CH] if out_ap is None else out_ap
        ores = r_RSTD[r] if out_res is None else out_res
        S.op("act", lambda e: e.activation(out=o, in_=PS[ps_i][:], func=AF.Sqrt,
                                           bias=cc(C_EPS + (0 if dim_eps > 5e-4 else 1)), scale=1.0),
             reads=[r_PS[ps_i], r_CONST], writes=[ores])
        S.op("dve", lambda e: e.reciprocal(out=o, in_=o), reads=[ores], writes=[ores])

    def sumsq_rstd(src_fn, src_res, nblk, ps_i, dimscale_eps, out_ap=None, out_res=None):
        for k in range(nblk):
            s = nxt("sq", 2)
            S.op("act", lambda e, k=k, s=s: e.activation(out=SQ[:, s * CH:(s + 1) * CH], in_=src_fn(k), func=AF.Square),
                 reads=[src_res(k)], writes=[r_SQ[s]])
            if SUB >= 2:
                S.op("pe", lambda e, k=k, s=s: e.matmul(PS[ps_i][:], ONES[:], SQ[:, s * CH:(s + 1) * CH],
                                                       start=(k == 0), stop=(k == nblk - 1)),
                     reads=[r_SQ[s], r_ONES], writes=[r_PS[ps_i]])
        r = nxt("rstd", 2)
        if SUB >= 3:
            rstd_from_ps(ps_i, r, dimscale_eps, out_ap, out_res)
        return r

    def prenorm(l, j, chunks=range(NCH)):
        for c in chunks:
            r = sumsq_rstd(lambda k, c=c: xs(k, c), lambda k, c=c: r_X[c], NB, 6 + (c % 2), D * EPS)
            for k in range(NB if SUB >= 5 else 0):
                t = nxt("tmp", 2)
                S.op("dve", lambda e, k=k, c=c, r=r, t=t: e.tensor_tensor(
                    out=TMP[:, t * CH:(t + 1) * CH], in0=xs(k, c), in1=RSTD[:, r * CH:(r + 1) * CH], op=ALU.mult),
                    reads=[r_X[c], r_RSTD[r]], writes=[r_TMP[t]])
                S.op("dve", lambda e, k=k, c=c, t=t: e.tensor_scalar(
                    out=hs(k, c), in0=TMP[:, t * CH:(t + 1) * CH], scalar1=gcol(l, j, k), scalar2=None, op0=ALU.mult),
                    reads=[r_TMP[t], r_CONST], writes=[r_HN[c]])

    def load_w8(slot, src_ap):
        S.op("pool", lambda e: e.dma_start(out=W8[:, slot * 1024:(slot + 1) * 1024], in_=src_ap),
             writes=[r_W8[slot]], dma=True)

    def load_w22(slot, src_ap):
        S.op("pool", lambda e: e.dma_start(out=W22[:, slot * FB * 128:(slot + 1) * FB * 128], in_=src_ap),
             writes=[r_W22[slot]], dma=True)

    def mm_group(ps_i, terms, reads):
        n = len(terms)

        def fn(e):
            ins = None
            for i, (a, b) in enumerate(terms):
                ins = e.matmul(PS[ps_i][:], a, b, start=(i == 0), stop=(i == n - 1))
            return ins
        S.op("pe", fn, reads=reads, writes=[r_PS[ps_i]])

    def ms(d, ci):
        o = d * 1024 + ci * CH
        return M[:, o:o + CH]

    def proj_post(l, j, chunk_pairs, wload, nk, lhs_fn, rhs_fn, rhs_res, slot_res, evac_scale=None, sq_scale=None, rhs_fn_d=None):
        if evac_scale is None:
            evac_scale = lambda d: gcol(l, j, d)
        for pair in chunk_pairs:
            for d in range(NB):
                slot = wload(d)
                for ci, c in enumerate(pair):
                    pi = 4 + ci
                    mm_group(pi, [(lhs_fn(slot, d, k), rhs_fn(k, c) if rhs_fn_d is None else rhs_fn_d(d, k, c)) for k in range(nk)],
                             reads=[slot_res(slot)] + rhs_res(c))
                    S.op("act", lambda e, d=d, ci=ci, pi=pi: e.activation(out=ms(d, ci), in_=PS[pi][:], func=AF.Copy,
                                                                       scale=evac_scale(d)),
                         reads=[r_PS[pi], r_CONST, r_SMALL], writes=[r_M[ci]])
                    s = nxt("sq", 2)
                    if sq_scale is None:
                        S.op("act", lambda e, pi=pi, s=s: e.activation(out=SQ[:, s * CH:(s + 1) * CH], in_=PS[pi][:],
                                                                    func=AF.Square),
                             reads=[r_PS[pi]], writes=[r_SQ[s]])
                    else:
                        S.op("act", lambda e, pi=pi, s=s, d=d: e.activation(out=SQ[:, s * CH:(s + 1) * CH], in_=PS[pi][:],
                                                                         func=AF.Square, scale=sq_scale(d)),
                             reads=[r_PS[pi], r_CONST], writes=[r_SQ[s]])
                    S.op("pe", lambda e, d=d, ci=ci, s=s: e.matmul(PS[6 + ci][:], ONES[:], SQ[:, s * CH:(s + 1) * CH],
                                                                start=(d == 0), stop=(d == NB - 1)),
                         reads=[r_SQ[s], r_ONES], writes=[r_PS[6 + ci]])
            for ci, c in enumerate(pair):
                r = nxt("rstd", 2)
                rstd_from_ps(6 + ci, r, D * EPS)
                for d in range(NB):
                    t = nxt("tmp", 2)
                    S.op("dve", lambda e, d=d, ci=ci, r=r, t=t: e.tensor_tensor(
                        out=TMP[:, t * CH:(t + 1) * CH], in0=ms(d, ci), in1=RSTD[:, r * CH:(r + 1) * CH], op=ALU.mult),
                        reads=[r_M[ci], r_RSTD[r]], writes=[r_TMP[t]])
                    S.op("dve", lambda e, d=d, c=c, t=t: e.tensor_tensor(
                        out=xs(d, c), in0=xs(d, c), in1=TMP[:, t * CH:(t + 1) * CH], op=ALU.add),
                        reads=[r_TMP[t], r_X[c]], writes=[r_X[c]])

    def w8_loader(src_fn):
        def wload(d):
            slot = nxt("w8", 6)
            load_w8(slot, src_fn(d))
            return slot
        return wload

    def a_s(f, ci):
        o = f * 1024 + ci * CH
        return BIG[:, o:o + CH]

    r_A = [[S.res("a%d_%d" % (f, ci)) for ci in range(2)] for f in range(FB)]

    def ffn(l):
        for hf in range(2):
            pair = (2 * hf, 2 * hf + 1)
            prenorm(l, 2, pair)
            for f in range(FB):
                sg_ = nxt("w8", 6)
                load_w8(sg_, wg_d[l][f])
                su_ = nxt("w8", 6)
                load_w8(su_, wu_d[l][f])
                for ci, c in enumerate(pair):
                    pg, pu = ci, 2 + ci
                    mm_group(pg, [(w8s(sg_, k), hs(k, c)) for k in range(NB)], reads=[r_W8[sg_], r_HN[c]])
                    mm_group(pu, [(w8s(su_, k), hs(k, c)) for k in range(NB)], reads=[r_W8[su_], r_HN[c]])
                    t = nxt("tmp", 2)
                    S.op("act", lambda e, pg=pg, t=t: e.activation(out=TMP[:, t * CH:(t + 1) * CH], in_=PS[pg][:], func=AF.Silu),
                         reads=[r_PS[pg]], writes=[r_TMP[t]])
                    S.op("dve", lambda e, f=f, ci=ci, pu=pu, t=t: e.tensor_tensor(
                        out=a_s(f, ci), in0=TMP[:, t * CH:(t + 1) * CH], in1=PS[pu][:], op=ALU.mult),
                        reads=[r_TMP[t], r_PS[pu]], writes=[r_A[f][ci]])

            def wload(d):
                slot = nxt("w22", 2)
                load_w22(slot, wd_d[l][d])
                return slot
            proj_post(l, 3, [pair], wload, FB,
                      lambda slot, d, k: w22s(slot, k),
                      lambda k, c, hf=hf: a_s(k, c - 2 * hf),
                      lambda c, hf=hf: [r_A[f][c - 2 * hf] for f in range(FB)],
                      lambda slot: r_W22[slot])

    r_fs = S.res("fence_scratch")

    def fence(reads, writes):
        S.op("dve", lambda e: e.memset(SMALL[:, 159:160], 0.0), reads=list(reads), writes=list(writes) + [r_fs])

    all_A = [r_A[f][ci] for f in range(FB) for ci in range(2)]

    def conv_mixer(l, ia):
        if STAGE < 2:
            return
        prenorm(l, 0)
        if STAGE < 3:
            return
        def U(s, a, n):
            o = s * T + a
            return M[:, o:o + n]

        def Bf(s, a, n):
            o = 2 * T + s * T + a
            return M[:, o:o + n]
        TT = BIG[:, 8 * T: 8 * T + 2 * T].bitcast(F32)
        r_U = [S.res("U0"), S.res("U1")]
        r_B = [S.res("B0"), S.res("B1")]
        T2 = BIG[:, 8 * T + 2 * T: 8 * T + 2 * T + 2048].bitcast(F32)
        r_TT = S.res("TT")
        r_T2 = S.res("T2")
        r_V = S.res("V")
        fence(all_A + r_M, r_U + r_B + [r_TT, r_T2, r_V])

        def cw(tap, f):
            return cc(C_CW + (ia * 3 + tap) * 8 + f)
        for f in range(NB):
            s3 = [nxt("w8", 6) for _ in range(3)]
            for i, sl in enumerate(s3):
                load_w8(sl, cwin_d[ia, i * 8 + f])
            us = f % 2
            for c in range(NCH):
                pb, pc, ph = (0, 1, 2) if c % 2 == 0 else (3, 4, 5)
                for pi, sl in zip((pb, pc, ph), s3):
                    mm_group(pi, [(w8s(sl, k), hs(k, c)) for k in range(NB)], reads=[r_W8[sl], r_HN[c]])
                t = nxt("tmp", 2)
                S.op("act", lambda e, ph=ph, t=t: e.activation(out=TMP[:, t * CH:(t + 1) * CH], in_=PS[ph][:], func=AF.Copy),
                     reads=[r_PS[ph]], writes=[r_TMP[t]])
                S.op("dve", lambda e, pc=pc, t=t, us=us, c=c: e.tensor_tensor(
                    out=U(us, c * CH, CH), in0=TMP[:, t * CH:(t + 1) * CH], in1=PS[pc][:], op=ALU.mult),
                    reads=[r_TMP[t], r_PS[pc]], writes=[r_U[us]])
                S.op("act", lambda e, pb=pb, us=us, c=c: e.activation(out=Bf(us, c * CH, CH), in_=PS[pb][:], func=AF.Copy),
                     reads=[r_PS[pb]], writes=[r_B[us]])
            S.op("dve", lambda e, us=us, f=f: e.tensor_scalar(out=TT, in0=U(us, 0, T), scalar1=cw(1, f), scalar2=None,
                                                           op0=ALU.mult),
                 reads=[r_U[us], r_CONST], writes=[r_TT])
            for tap, (oa, ob, ia_) in ((0, (1, 1024, 0)), (0, (1024, T, 1023)), (2, (0, 1024, 1)), (2, (1024, T - 1, 1025))):
                n = ob - oa
                S.op("act", lambda e, us=us, f=f, tap=tap, ia_=ia_, n=n: e.activation(
                    out=T2[:, 0:n], in_=U(us, ia_, n), func=AF.Copy, scale=cw(tap, f)),
                    reads=[r_U[us], r_CONST], writes=[r_T2])
                S.op("dve", lambda e, oa=oa, ob=ob, n=n: e.tensor_tensor(out=TT[:, oa:ob], in0=TT[:, oa:ob], in1=T2[:, 0:n],
                                                                       op=ALU.add),
                     reads=[r_T2, r_TT], writes=[r_TT])
            S.op("dve", lambda e, us=us, f=f: e.tensor_tensor(out=BIG[:, f * T:(f + 1) * T], in0=TT, in1=Bf(us, 0, T),
                                                            op=ALU.mult),
                 reads=[r_TT, r_B[us]], writes=[r_V])
            for j, tcol in enumerate((0, T - 1)):
                S.op("dve", lambda e, us=us, f=f, j=j, tcol=tcol: e.tensor_copy(
                    out=SMALL[:, 2 * f + j:2 * f + j + 1], in_=U(us, tcol, 1)), reads=[r_U[us]], writes=[r_SMALL])
                S.op("dve", lambda e, f=f, j=j, tcol=tcol: e.tensor_copy(
                    out=SMALL[:, 16 + 2 * f + j:16 + 2 * f + j + 1], in_=TT[:, tcol:tcol + 1]), reads=[r_TT], writes=[r_SMALL])
                S.op("dve", lambda e, us=us, f=f, j=j, tcol=tcol: e.tensor_copy(
                    out=SMALL[:, 32 + 2 * f + j:32 + 2 * f + j + 1], in_=Bf(us, tcol, 1)), reads=[r_B[us]], writes=[r_SMALL])
        if STAGE < 4:
            return
        S.op("sp", lambda e: e.dma_start(out=edge_in[:, 0:16], in_=SMALL[:, 0:16]), reads=[r_SMALL], writes=[r_edge_in], dma=True)
        halo_exchange()
        S.op("sp", lambda e: e.dma_start(out=SMALL[:, 48:80].rearrange("p (r n) -> p r n", r=2),
                                         in_=edge_out.ap().rearrange("(r p) n -> p r n", p=128)[:, :, 0:16]),
             reads=[r_edge_out], writes=[r_SMALL], dma=True)
        sm3 = lambda o: SMALL[:, o:o + 16].rearrange("p (f two) -> p f two", two=2)
        cw3 = lambda tap: CONST[:, C_CW + (ia * 3 + tap) * 8: C_CW + (ia * 3 + tap) * 8 + 8]
        for side, (hoff, hidx, mcol, tap, tcol) in enumerate(((48, 1, C_ML, 0, 0), (64, 0, C_MR, 2, T - 1))):
            hl = SMALL[:, 80 + 8 * side: 88 + 8 * side]
            S.op("dve", lambda e, hl=hl, hoff=hoff, hidx=hidx, mcol=mcol: e.tensor_scalar(
                out=hl, in0=sm3(hoff)[:, :, hidx], scalar1=cc(mcol), scalar2=None, op0=ALU.mult),
                reads=[r_SMALL, r_CONST], writes=[r_SMALL])
            S.op("dve", lambda e, hl=hl, tap=tap: e.tensor_tensor(out=hl, in0=hl, in1=cw3(tap), op=ALU.mult),
                 reads=[r_SMALL, r_CONST], writes=[r_SMALL])
            S.op("dve", lambda e, hl=hl, side=side: e.tensor_tensor(out=hl, in0=hl, in1=sm3(16)[:, :, side], op=ALU.add),
                 reads=[r_SMALL], writes=[r_SMALL])
            S.op("dve", lambda e, hl=hl, side=side, tcol=tcol: e.tensor_tensor(
                out=BIG[:, 0:NB * T].rearrange("p (f t) -> p f t", f=NB)[:, :, tcol],
                in0=hl, in1=sm3(32)[:, :, side], op=ALU.mult),
                reads=[r_SMALL, r_V], writes=[r_V])
        fence(r_U + r_B + [r_TT, r_T2], r_M)
        proj_post(l, 1, [(1, 2), (0, 3)], w8_loader(lambda d: cwout_d[ia, d]), NB,
                  lambda slot, d, k: w8s(slot, k),
                  lambda k, c: BIG[:, k * T + c * CH: k * T + (c + 1) * CH],
                  lambda c: [r_V],
                  lambda slot: r_W8[slot])
        fence([r_V, r_TT, r_T2], all_A)


    def pool_mixer(l):
        HW = T + 16
        RS_ALL = BIG[:, 0:2 * T].bitcast(F32)
        PEDGE = M[:, 0:128]
        PHALO = M[:, 128:384]
        HP = M[:, 384:384 + HW]
        SA = M[:, 384 + HW:384 + 2 * HW]
        SB = M[:, 384 + 2 * HW:384 + 3 * HW]
        r_RS, r_PE, r_PH, r_HP, r_SA, r_SB = (S.res(n) for n in ("RSALL", "PEDGE", "PHALO", "HP", "SA", "SB"))
        r_pin, r_pout, r_PW = S.res("pedge_in"), S.res("pedge_out"), S.res("PW")
        fence(all_A + r_M, [r_RS, r_PE, r_PH, r_HP, r_SA, r_SB])
        PW = W22[:, 0:2048]
        S.op("pool", lambda e: e.dma_start(out=PW, in_=pw_d.ap()), writes=[r_PW] + r_W22, dma=True)
        S.op("dve", lambda e: e.tensor_tensor(out=SMALL[:, 100:108], in0=cc(C_PS, 8), in1=cc(C_G + (l * 4 + 1) * 8, 8), op=ALU.mult),
             reads=[r_CONST, r_SMALL], writes=[r_SMALL])
        for c in range(NCH):
            sumsq_rstd(lambda k, c=c: xs(k, c), lambda k, c=c: r_X[c], NB, 6 + (c % 2), D * EPS,
                       out_ap=RS_ALL[:, c * CH:(c + 1) * CH], out_res=r_RS)
        for f in range(NB):
            for side, a in enumerate((0, T - 8)):
                o = f * 16 + side * 8
                S.op("dve", lambda e, f=f, a=a, o=o: e.tensor_tensor(out=PEDGE[:, o:o + 8], in0=X[:, f * T + a:f * T + a + 8],
                                                                  in1=RS_ALL[:, a:a + 8], op=ALU.mult),
                     reads=[r_X[0], r_X[3], r_RS], writes=[r_PE])
                S.op("dve", lambda e, f=f, o=o: e.tensor_scalar(out=PEDGE[:, o:o + 8], in0=PEDGE[:, o:o + 8],
                                                             scalar1=gcol(l, 0, f), scalar2=None, op0=ALU.mult),
                     reads=[r_PE, r_CONST], writes=[r_PE])
        S.op("sp", lambda e: e.dma_start(out=pedge_in[:, :], in_=PEDGE), reads=[r_PE], writes=[r_pin], dma=True)
        S.op("pool", lambda e: e.collective_compute("AllGather", ALU.bypass, replica_groups=PAIRS,
                                                    ins=[pedge_in.ap().opt()], outs=[pedge_out.ap().opt()]),
             reads=[r_pin], writes=[r_pout])
        S.op("sp", lambda e: e.dma_start(out=PHALO.rearrange("p (r n) -> p r n", r=2),
                                         in_=pedge_out.ap().rearrange("(r p) n -> p r n", p=128)),
             reads=[r_pout], writes=[r_PH], dma=True)
        for f in range(NB):
            g = f // 2
            w = POOLW[g]
            S.op("dve", lambda e, f=f: e.tensor_scalar(out=HP[:, 0:8], in0=PHALO[:, f * 16 + 8:f * 16 + 16], scalar1=cc(C_ML),
                                                     scalar2=None, op0=ALU.mult), reads=[r_PH, r_CONST], writes=[r_HP])
            S.op("dve", lambda e, f=f: e.tensor_scalar(out=HP[:, 8 + T:16 + T], in0=PHALO[:, 128 + f * 16:128 + f * 16 + 8],
                                                     scalar1=cc(C_MR), scalar2=None, op0=ALU.mult),
                 reads=[r_PH, r_CONST], writes=[r_HP])
            S.op("dve", lambda e, f=f: e.tensor_tensor(out=HP[:, 8:8 + T], in0=X[:, f * T:(f + 1) * T], in1=RS_ALL, op=ALU.mult),
                 reads=r_X + [r_RS], writes=[r_HP])
            S.op("dve", lambda e, f=f: e.tensor_scalar(out=HP[:, 8:8 + T], in0=HP[:, 8:8 + T], scalar1=gcol(l, 0, f),
                                                     scalar2=None, op0=ALU.mult), reads=[r_HP, r_CONST], writes=[r_HP])
            S.op("dve", lambda e: e.tensor_tensor(out=SA[:, 1:HW], in0=HP[:, 0:HW - 1], in1=HP[:, 1:HW], op=ALU.add),
                 reads=[r_HP], writes=[r_SA])
            cur, cur_r, oth, oth_r = SA, r_SA, SB, r_SB
            if w >= 4:
                S.op("dve", lambda e: e.tensor_tensor(out=SB[:, 2:HW - 1], in0=SA[:, 1:HW - 2], in1=SA[:, 3:HW], op=ALU.add),
                     reads=[r_SA], writes=[r_SB])
                cur, cur_r, oth, oth_r = SB, r_SB, SA, r_SA
            if w >= 8:
                S.op("dve", lambda e: e.tensor_tensor(out=SA[:, 4:HW - 3], in0=SB[:, 2:HW - 5], in1=SB[:, 6:HW - 1], op=ALU.add),
                     reads=[r_SB], writes=[r_SA])
                cur, cur_r, oth, oth_r = SA, r_SA, SB, r_SB
            if w >= 16:
                S.op("dve", lambda e: e.tensor_tensor(out=SB[:, 8:HW - 7], in0=SA[:, 4:HW - 11], in1=SA[:, 12:HW - 3], op=ALU.add),
                     reads=[r_SA], writes=[r_SB])
                cur, cur_r, oth, oth_r = SB, r_SB, SA, r_SA
            S.op("dve", lambda e, cur=cur, g=g: e.tensor_tensor(out=cur[:, 8:16], in0=cur[:, 8:16], in1=cc(C_CL + g * 8, 8), op=ALU.mult),
                 reads=[cur_r, r_CONST], writes=[cur_r])
            S.op("dve", lambda e, cur=cur, g=g: e.tensor_tensor(out=cur[:, T:T + 8], in0=cur[:, T:T + 8], in1=cc(C_CR + g * 8, 8), op=ALU.mult),
                 reads=[cur_r, r_CONST], writes=[cur_r])
            S.op("act", lambda e, cur=cur, oth=oth, w=w: e.activation(out=oth[:, 8:8 + T], in_=cur[:, 8:8 + T], func=AF.Copy,
                                                                    scale=1.0 / w), reads=[cur_r], writes=[oth_r])
            S.op("dve", lambda e, oth=oth, f=f: e.tensor_tensor(out=HN[:, f * T:(f + 1) * T], in0=oth[:, 8:8 + T], in1=HP[:, 8:8 + T],
                                                              op=ALU.subtract), reads=[oth_r, r_HP], writes=r_HN)
        fence([r_PE, r_PH, r_HP, r_SA, r_SB], r_M)

        def lhs(slot, d, k):
            g, dblk = d // 2, d % 2
            o = (g * 2 + k) * 256 + dblk * 128
            return PW[:, o:o + 128]
        proj_post(l, 1, [(0, 1), (2, 3)], lambda d: 0, 2, lhs,
                  lambda k, c: None, lambda c: [r_HN[c]], lambda slot: r_PW,
                  evac_scale=lambda d: SMALL[:, 100 + d:101 + d], sq_scale=lambda d: cc(C_PS + d),
                  rhs_fn_d=lambda d, k, c: hs((d // 2) * 2 + k, c))
        fence([r_RS, r_PW], all_A + r_W22)

    def attn_mixer(l):
        li = lambda_init(l)
        MB = M[:].bitcast(BF16)
        WF = W22[:].bitcast(F32)
        r_lam = S.res("lamtmp")
        fence(r_TMP, [r_lam])
        S.op("sp", lambda e: e.dma_start(out=TMP[:, 0:512], in_=lamv_d.ap()), writes=[r_lam, r_TMP[0]], dma=True)
        for i in range(2):
            S.op("dve", lambda e, i=i: e.tensor_tensor(out=TMP[:, 512 + 128 * i:640 + 128 * i], in0=TMP[:, 256 * i:256 * i + 128],
                                                     in1=TMP[:, 256 * i + 128:256 * i + 256], op=ALU.mult),
                 reads=[r_lam], writes=[r_TMP[1]])
            S.op("dve", lambda e, i=i: e.reduce_sum(out=SMALL[:, 110 + i:111 + i], in_=TMP[:, 512 + 128 * i:640 + 128 * i],
                                                  axis=mybir.AxisListType.X), reads=[r_TMP[1]], writes=[r_SMALL])
        S.op("act", lambda e: e.activation(out=SMALL[:, 112:114], in_=SMALL[:, 110:112], func=AF.Exp), reads=[r_SMALL], writes=[r_SMALL])
        S.op("dve", lambda e: e.tensor_tensor(out=SMALL[:, 114:115], in0=SMALL[:, 112:113], in1=SMALL[:, 113:114], op=ALU.subtract),
             reads=[r_SMALL], writes=[r_SMALL])
        S.op("dve", lambda e: e.tensor_scalar(out=SMALL[:, 115:116], in0=SMALL[:, 114:115], scalar1=li, scalar2=-1.0,
                                            op0=ALU.add, op1=ALU.mult), reads=[r_SMALL], writes=[r_SMALL])
        S.op("dve", lambda e: e.tensor_scalar(out=SMALL[:, 116:118], in0=cc(C_SG, 2), scalar1=16.0 * (1.0 - li), scalar2=None,
                                            op0=ALU.mult), reads=[r_SMALL, r_CONST], writes=[r_SMALL])
        fence([r_lam], r_TMP)
        neglam = SMALL[:, 115:116]
        prenorm(l, 0)
        r_Q = S.res("QT")
        r_KST = [S.res("kst0"), S.res("kst1")]
        r_VST = [S.res("vst0"), S.res("vst1")]
        r_WV = S.res("WV")
        r_ktin = [S.res("kt_in0"), S.res("kt_in1")]
        r_ktall = [S.res("kt_all0"), S.res("kt_all1")]
        r_vin = [S.res("v_in0"), S.res("v_in1")]
        r_vall = [S.res("v_all0"), S.res("v_all1")]

        def gather(src, dst, r_src, r_dst):
            S.op("pool", lambda e: e.collective_compute("AllGather", ALU.bypass, replica_groups=PAIRS,
                                                        ins=[src.ap().opt()], outs=[dst.ap().opt()]),
                 reads=[r_src], writes=[r_dst])
        fence(all_A + r_M, [r_Q, r_WV] + r_KST + r_VST)
        WV = MB[:, 4096:12288]
        for i in range(2):
            S.op("pool", lambda e, i=i: e.dma_start(out=WV[:, i * 4096:(i + 1) * 4096], in_=av_d.ap()[:, i * 4096:(i + 1) * 4096]),
                 writes=[r_WV], dma=True)
        qscale = 128 ** -0.5
        bank = [0]

        def nb():
            bank[0] = (bank[0] + 1) % 4
            return bank[0]
        for j in range(NB):
            sl = nxt("w8", 6)
            load_w8(sl, aqk_d[j])
            for c in range(NCH):
                pi = nb()
                mm_group(pi, [(w8s(sl, k), hs(k, c)) for k in range(NB)], reads=[r_W8[sl], r_HN[c]])
                S.op("act", lambda e, pi=pi, j=j, c=c: e.activation(out=BIG[:, j * T + c * CH:j * T + (c + 1) * CH], in_=PS[pi][:],
                                                                  func=AF.Copy, scale=qscale), reads=[r_PS[pi]], writes=[r_Q])
        for j in range(NB):
            sl = nxt("w8", 6)
            load_w8(sl, aqk_d[8 + j])
            ks = j % 2
            for c in range(NCH):
                pi = nb()
                mm_group(pi, [(w8s(sl, k), hs(k, c)) for k in range(NB)], reads=[r_W8[sl], r_HN[c]])
                S.op("act", lambda e, pi=pi, ks=ks, c=c: e.activation(out=MB[:, ks * T + c * CH:ks * T + (c + 1) * CH], in_=PS[pi][:],
                                                                   func=AF.Copy), reads=[r_PS[pi]], writes=[r_KST[ks]])
            S.op("sp", lambda e, j=j, ks=ks: e.dma_start(out=kt_in[j // 4][(j % 4) * 128:(j % 4 + 1) * 128, :],
                                                       in_=MB[:, ks * T:(ks + 1) * T]),
                 reads=[r_KST[ks]], writes=[r_ktin[j // 4]], dma=True)
            if j % 4 == 3:
                gather(kt_in[j // 4], kt_all[j // 4], r_ktin[j // 4], r_ktall[j // 4])
        for tb in range(16):
            vs = tb % 2
            for ec in range(2):
                pi = nb()
                mm_group(pi, [(HN[:, k * T + tb * 128:k * T + (tb + 1) * 128], WV[:, k * 1024 + ec * 512:k * 1024 + (ec + 1) * 512])
                              for k in range(NB)], reads=[r_WV] + r_HN)
                S.op("act", lambda e, pi=pi, vs=vs, ec=ec: e.activation(
                    out=MB[:, 12288 + vs * 1024 + ec * 512:12288 + vs * 1024 + (ec + 1) * 512], in_=PS[pi][:], func=AF.Copy),
                    reads=[r_PS[pi]], writes=[r_VST[vs]])
            S.op("sp", lambda e, tb=tb, vs=vs: e.dma_start(out=v_in[tb // 8][(tb % 8) * 128:(tb % 8 + 1) * 128, :],
                                                         in_=MB[:, 12288 + vs * 1024:12288 + (vs + 1) * 1024]),
                 reads=[r_VST[vs]], writes=[r_vin[tb // 8]], dma=True)
            if tb % 8 == 7:
                gather(v_in[tb // 8], v_all[tb // 8], r_vin[tb // 8], r_vall[tb // 8])
        r_KT = [S.res("KT0"), S.res("KT1")]
        r_V = S.res("Vh")
        r_ST = S.res("strips")
        r_ON0, r_OD = S.res("On0"), S.res("Od")
        r_P = [S.res("P%d" % i) for i in range(3)]
        fence(r_KST + r_VST + [r_WV] + r_W22, r_KT + [r_V, r_ON0, r_OD] + r_P)
        fence(all_A, [r_ST])
        STRIP = BIG[:, 8 * T:8 * T + 2 * 2304].bitcast(F32)
        KT = lambda t: MB[:, t * 4096:(t + 1) * 4096]
        VH = MB[:, 8192:16384]
        ON0 = lambda eb: WF[:, eb * CH:(eb + 1) * CH]
        OD = lambda eb: WF[:, 1024 + eb * CH:1024 + (eb + 1) * CH]
        PT = lambda i: WF[:, 2048 + 256 * i:2048 + 256 * (i + 1)].bitcast(BF16)
        prot = [0]
        for h in range(4):
            S.op("sp", lambda e, h=h: e.dma_start(out=STRIP, in_=strips_d[h]), writes=[r_ST], dma=True)
            for r in range(2):
                for hv in range(2):
                    b0 = r * 16 + hv * 8
                    S.op("sp", lambda e, h=h, r=r, hv=hv, b0=b0: e.dma_start(
                        out=VH[:, b0 * 256:(b0 + 8) * 256].rearrange("p (b e) -> p b e", e=256),
                        in_=v_all[hv].ap().rearrange("(b p) e -> p b e", p=128)[:, r * 8:(r + 1) * 8, h * 256:(h + 1) * 256]),
                        reads=[r_vall[hv]], writes=[r_V], dma=True)
            for t in range(2):
                j = 2 * h + t
                for r in range(2):
                    S.op("sp", lambda e, t=t, j=j, r=r: e.dma_start(
                        out=KT(t)[:, r * T:(r + 1) * T],
                        in_=kt_all[j // 4][r * 512 + (j % 4) * 128:r * 512 + (j % 4 + 1) * 128, :]),
                        reads=[r_ktall[j // 4]], writes=[r_KT[t]], dma=True)
            for qc in range(NCH):
                for t in range(2):
                    j = 2 * h + t
                    for kb in range(32):
                        sbk = kb % 2
                        mm_group(sbk, [(KT(t)[:, kb * 128:(kb + 1) * 128], BIG[:, j * T + qc * CH:j * T + (qc + 1) * CH])],
                                 reads=[r_KT[t], r_Q])
                        pi = prot[0]
                        prot[0] = (pi + 1) % 3
                        da, dbb = kb - 4 * qc + 1, kb - 16 - 4 * qc + 1
                        if 0 <= da <= 5 or 0 <= dbb <= 5:
                            off = (5 - da) * 128 if 0 <= da <= 5 else 1152 + (5 - dbb) * 128
                            tt = nxt("tmp", 2)
                            S.op("dve", lambda e, sbk=sbk, off=off, tt=tt: e.tensor_tensor(
                                out=TMP[:, tt * CH:(tt + 1) * CH], in0=PS[sbk][:], in1=STRIP[:, off:off + CH], op=ALU.add),
                                reads=[r_PS[sbk], r_ST], writes=[r_TMP[tt]])
                            S.op("act", lambda e, tt=tt, pi=pi: e.activation(out=PT(pi), in_=TMP[:, tt * CH:(tt + 1) * CH], func=AF.Exp),
                                 reads=[r_TMP[tt]], writes=[r_P[pi]])
                        else:
                            cls = 0 if da < 0 else (2 if dbb > 5 else 1)
                            S.op("act", lambda e, sbk=sbk, pi=pi, h=h, cls=cls: e.activation(
                                out=PT(pi), in_=PS[sbk][:], func=AF.Exp, bias=cc(C_BC + h * 3 + cls), scale=1.0),
                                reads=[r_PS[sbk], r_CONST], writes=[r_P[pi]])
                        for eb in range(2):
                            S.op("pe", lambda e, eb=eb, kb=kb, pi=pi: e.matmul(
                                PS[2 + eb][:], VH[:, kb * 256 + eb * 128:kb * 256 + (eb + 1) * 128], PT(pi),
                                start=(kb == 0), stop=(kb == 31)), reads=[r_V, r_P[pi]], writes=[r_PS[2 + eb]])
                        S.op("pe", lambda e, kb=kb, pi=pi: e.matmul(PS[4][:], ONES[:], PT(pi), start=(kb == 0), stop=(kb == 31)),
                             reads=[r_ONES, r_P[pi]], writes=[r_PS[4]])
                    rr = nxt("rstd", 2)
                    S.op("dve", lambda e, rr=rr: e.reciprocal(out=RSTD[:, rr * CH:(rr + 1) * CH], in_=PS[4][:]),
                         reads=[r_PS[4]], writes=[r_RSTD[rr]])
                    for eb in range(2):
                        if t == 0:
                            S.op("dve", lambda e, eb=eb, rr=rr: e.tensor_tensor(out=ON0(eb), in0=PS[2 + eb][:],
                                                                              in1=RSTD[:, rr * CH:(rr + 1) * CH], op=ALU.mult),
                                 reads=[r_PS[2 + eb], r_RSTD[rr]], writes=[r_ON0])
                        else:
                            tt = nxt("tmp", 2)
                            S.op("dve", lambda e, eb=eb, rr=rr, tt=tt: e.tensor_tensor(
                                out=TMP[:, tt * CH:(tt + 1) * CH], in0=PS[2 + eb][:], in1=RSTD[:, rr * CH:(rr + 1) * CH], op=ALU.mult),
                                reads=[r_PS[2 + eb], r_RSTD[rr]], writes=[r_TMP[tt]])
                            S.op("dve", lambda e, tt=tt: e.tensor_scalar(out=TMP[:, tt * CH:(tt + 1) * CH], in0=TMP[:, tt * CH:(tt + 1) * CH],
                                                                       scalar1=neglam, scalar2=None, op0=ALU.mult),
                                 reads=[r_TMP[tt], r_SMALL], writes=[r_TMP[tt]])
                            S.op("dve", lambda e, eb=eb, tt=tt: e.tensor_tensor(out=OD(eb), in0=TMP[:, tt * CH:(tt + 1) * CH], in1=ON0(eb),
                                                                              op=ALU.add), reads=[r_TMP[tt], r_ON0], writes=[r_OD])
                rr = sumsq_rstd(lambda eb: OD(eb), lambda eb: r_OD, 2, 5, 256 * EPS)
                for eb in range(2):
                    tt = nxt("tmp", 2)
                    S.op("dve", lambda e, eb=eb, rr=rr, tt=tt: e.tensor_tensor(out=TMP[:, tt * CH:(tt + 1) * CH], in0=OD(eb),
                                                                             in1=RSTD[:, rr * CH:(rr + 1) * CH], op=ALU.mult),
                         reads=[r_OD, r_RSTD[rr]], writes=[r_TMP[tt]])
                    S.op("dve", lambda e, eb=eb, tt=tt, h=h, qc=qc: e.tensor_scalar(
                        out=hs(2 * h + eb, qc), in0=TMP[:, tt * CH:(tt + 1) * CH], scalar1=SMALL[:, 116 + eb:117 + eb], scalar2=None,
                        op0=ALU.mult), reads=[r_TMP[tt], r_SMALL], writes=[r_HN[qc]])
        fence(r_KT + [r_V, r_ON0, r_OD] + r_P, r_M + r_W22)
        proj_post(l, 1, [(0, 1), (2, 3)], w8_loader(lambda d: ao_d[d]), NB,
                  lambda slot, d, k: w8s(slot, k), lambda k, c: hs(k, c), lambda c: [r_HN[c]], lambda slot: r_W8[slot])
        fence([r_Q, r_ST], all_A)

    r_cc = S.res("cc")

    def halo_exchange():
        S.op("pool", lambda e: e.collective_compute("AllGather", ALU.bypass, replica_groups=PAIRS,
                                                    ins=[edge_in.ap().opt()], outs=[edge_out.ap().opt()]),
             reads=[r_edge_in], writes=[r_edge_out])

    for l in layers:
        kind = l % 3
        if kind == 0:
            conv_mixer(l, l // 3)
        elif kind == 1:
            pool_mixer(l)
        else:
            attn_mixer(l)
        if STAGE >= 5:
            ffn(l)

    r_out = [S.res("out%d" % c) for c in range(NCH)]
    evs = []
    for c in range(NCH):
        evs.append(S.op("sp", lambda e, c=c: e.dma_start(
            out=y_out.rearrange("p (k t) -> p k t", k=NB)[:, :, c * CH:(c + 1) * CH],
            in_=X[:].rearrange("p (k t) -> p k t", k=NB)[:, :, c * CH:(c + 1) * CH]),
            reads=[r_X[c]], writes=[r_out[c]], dma=True))
    S.final_wait("sp", evs)
    S.emit()
    es.close()
    return nc


def _tile_kn(w, kb):
    K, N = w.shape
    return np.ascontiguousarray(w.reshape(kb, 128, N // 128, 128).transpose(2, 1, 0, 3).reshape(N // 128, 128, kb * 128))


def _pcol(v):
    return v.reshape(-1, 128).T


def _t5_bucket_np(rel):
    half = 16
    ret = np.where(rel > 0, half, 0)
    n = np.abs(rel)
    max_exact = 8
    nf = np.maximum(n, 1).astype(np.float32)
    large = max_exact + (np.log(nf / max_exact) / math.log(128 / max_exact) * (half - max_exact)).astype(np.int32)
    large = np.minimum(large, half - 1)
    return ret + np.where(n < max_exact, n, large)


def prep_inputs(inp):
    f32 = np.float32
    x = np.asarray(inp["x"], f32)
    shared = {}
    for l in range(DEPTH):
        shared["wg%d" % l] = _tile_kn(np.asarray(inp["ffn_w_gate"][l], f32), 8)
        shared["wu%d" % l] = _tile_kn(np.asarray(inp["ffn_w_up"][l], f32), 8)
        shared["wd%d" % l] = _tile_kn(np.asarray(inp["ffn_w_down"][l], f32), FB)
    shared["cwin"] = np.stack([_tile_kn(np.asarray(inp["conv_w_in"][i], f32), 8) for i in range(2)])
    shared["cwout"] = np.stack([_tile_kn(np.asarray(inp["conv_w_out"][i], f32), 8) for i in range(2)])
    pw = np.asarray(inp["pool_w"][0], f32)
    shared["pw"] = np.ascontiguousarray(pw.reshape(4, 2, 128, 256).transpose(2, 0, 1, 3).reshape(128, 4 * 2 * 256))
    wqkv = np.asarray(inp["attn_w_qkv"][0], f32)
    shared["aqk"] = _tile_kn(wqkv[:, :2048], 8)
    shared["av"] = np.ascontiguousarray(wqkv[:, 2048:].reshape(8, 128, 1024).transpose(1, 0, 2).reshape(128, 8 * 1024))
    shared["ao"] = _tile_kn(np.asarray(inp["attn_w_o"][0], f32), 8)
    lamv = np.concatenate([np.asarray(inp[k][0], f32) for k in ("lambda_q1", "lambda_k1", "lambda_q2", "lambda_k2")])
    shared["lamv"] = np.ascontiguousarray(np.broadcast_to(lamv[None, :], (128, 512)))
    rel_bias = np.asarray(inp["rel_bias"], f32)
    maps = []
    for c in range(8):
        b, half = c // 2, c % 2
        m = dict(shared)
        xc = x[b, half * T:(half + 1) * T, :]
        m["x_in"] = np.ascontiguousarray(xc.T.reshape(NB, 128, T).transpose(1, 0, 2).reshape(128, NB * T))
        cst = np.zeros((128, NCONST), f32)
        ng = np.asarray(inp["norm_g"], f32)
        for l in range(DEPTH):
            for j in range(4):
                cst[:, C_G + (l * 4 + j) * 8: C_G + (l * 4 + j) * 8 + 8] = _pcol(ng[l, j])
        cwv = np.asarray(inp["conv_w"], f32)
        for ia in range(2):
            for tap in range(3):
                o = C_CW + (ia * 3 + tap) * 8
                cst[:, o:o + 8] = _pcol(cwv[ia, tap])
        cst[:, C_PS:C_PS + 8] = _pcol(np.asarray(inp["pool_scale"][0], f32))
        cst[:, C_SG:C_SG + 2] = _pcol(np.asarray(inp["attn_subln_g"][0], f32))
        cst[:, C_ML] = 1.0 if half == 1 else 0.0
        cst[:, C_MR] = 1.0 if half == 0 else 0.0
        for g, w in enumerate(POOLW):
            for i in range(8):
                for side, tl in ((0, i), (1, T - 8 + i)):
                    t = half * T + tl
                    lo = max(t - w // 2, 0)
                    hi = min(t + w - 1 - w // 2, SEQ - 1)
                    cst[:, (C_CL if side == 0 else C_CR) + g * 8 + i] = w / (hi - lo + 1)
        for h in range(4):
            cst[:, C_BC + h * 3 + 0] = rel_bias[15, h]
            cst[:, C_BC + h * 3 + 2] = rel_bias[31, h]
            cst[:, C_BC + h * 3 + 1] = rel_bias[31, h] if half == 0 else rel_bias[15, h]
        cst[:, C_EPS] = D * EPS
        cst[:, C_EPS + 1] = 256 * EPS
        m["consts"] = cst
        kl = np.arange(128)[:, None]
        y = np.arange(1152)[None, :]
        rel = kl - y + 4 * 128
        bk = _t5_bucket_np(rel)
        st = np.zeros((4, 128, 2 * 1152), f32)
        for h in range(4):
            near = rel_bias[bk, h]
            if half == 0:
                st[h, :, :1152] = near
                st[h, :, 1152:] = rel_bias[31, h]
            else:
                st[h, :, :1152] = rel_bias[15, h]
                st[h, :, 1152:] = near
        m["strips"] = st
        maps.append(m)
    return maps


_NC_CACHE = {}


def run_layers(layers, maps):
    key = tuple(layers)
    nc = build(layers)
    maps = [{k: m[k] for k in nc._used_inputs} for m in maps]
    res = run_bass_kernel_spmd(nc, maps, core_ids=list(range(8)))
    return [r["y_out"] for r in res.results]


def assemble(ys):
    out = np.zeros((4, SEQ, D), np.float32)
    for c in range(8):
        b, half = c // 2, c % 2
        yc = ys[c].reshape(128, NB, T).transpose(1, 0, 2).reshape(D, T).T
        out[b, half * T:(half + 1) * T, :] = yc
    return out


def kernel(**inputs):
    maps = prep_inputs(inputs)
    ys = run_layers(list(range(DEPTH)), maps)
    return assemble(ys)
```

```python
import math
from contextlib import ExitStack
import numpy as np
import concourse.bass as bass
import concourse.mybir as mybir
from concourse.bass_utils import run_bass_kernel_spmd

F32 = mybir.dt.float32
BF16 = mybir.dt.bfloat16
AF = mybir.ActivationFunctionType
ALU = mybir.AluOpType

D = 1024
SEQ = 4096
T = 2048
NB = 8
CH = 512
NCH = 4
DFF = 2816
FB = 22
EPS = 1e-6
DEPTH = 4
PAIRS = [[0, 1], [2, 3], [4, 5], [6, 7]]
POOLW = (2, 4, 8, 16)

C_G = 0
C_CW = 128
C_PS = 176
C_SG = 184
C_ML = 186
C_MR = 187
C_CL = 188
C_CR = 220
C_BC = 252
C_EPS = 264
NCONST = 266


def lambda_init(i):
    return 0.8 - 0.6 * math.exp(-0.3 * i)


class Ev:
    __slots__ = ("sem", "val")

    def __init__(self, sem, val):
        self.sem = sem
        self.val = val


class Res:
    def __init__(self, name):
        self.name = name
        self.w = None
        self.rs = {}
        self.dsem = None
        self.dcnt = 0


ENGS = ("pe", "act", "dve", "pool", "sp")


class Sched:
    def __init__(self, nc, es):
        self.nc = nc
        self.es = es
        self.q = {e: [] for e in ENGS}
        self.cnt = {e: 0 for e in ENGS}
        self.esem = {e: es.enter_context(nc.semaphore("es_" + e)) for e in ENGS}
        self.seen = {e: {} for e in ENGS}
        self.nsem = 0

    def res(self, name):
        return Res(name)

    def op(self, eng, fn, reads=(), writes=(), dma=False, inc=True):
        waits = {}

        def need(ev):
            if ev is not None:
                k = id(ev.sem)
                if k not in waits or waits[k].val < ev.val:
                    waits[k] = ev

        own = self.esem[eng]
        for r in reads:
            need(r.w)
        for w in writes:
            if not (eng == "pe" and w.w is not None and w.w.sem is own):
                need(w.w)
            for e in w.rs.values():
                need(e)
        seen = self.seen[eng]
        ws = []
        for k, ev in waits.items():
            if seen.get(k, 0) < ev.val:
                seen[k] = ev.val
                ws.append((ev.sem, ev.val))
        if dma:
            dst = writes[0]
            if dst.dsem is None:
                dst.dsem = self.es.enter_context(self.nc.semaphore("d%d_%s" % (self.nsem, dst.name)))
                self.nsem += 1
            dst.dcnt += 16
            ev = Ev(dst.dsem, dst.dcnt)
            amt = 16
        else:
            self.cnt[eng] += 1
            ev = Ev(own, self.cnt[eng])
            amt = 1
        for r in reads:
            r.rs[id(ev.sem)] = ev
        for w in writes:
            w.w = ev
            w.rs = {}
        self.q[eng].append((ws, fn, ev.sem, amt))
        return ev

    def final_wait(self, eng, evs):
        ws = [(e.sem, e.val) for e in evs]
        self.q[eng].append((ws, None, None, 0))

    def emit(self):
        nc = self.nc
        q = self.q

        def run(e, lst):
            for ws, fn, sem, amt in lst:
                for s, v in ws:
                    e.wait_ge(s, v)
                if fn is not None:
                    fn(e).then_inc(sem, amt)

        with nc.Block() as block:
            @block.tensor
            def _(e):
                run(e, q["pe"])

            @block.scalar
            def _(e):
                run(e, q["act"])

            @block.vector
            def _(e):
                run(e, q["dve"])

            @block.gpsimd
            def _(e):
                run(e, q["pool"])

            @block.sync
            def _(e):
                run(e, q["sp"])


STAGE = 99
SUB = 99


def build(layers, has_attn=True):
    nc = bass.Bass("TRN2", target_bir_lowering=False)
    es = ExitStack()
    S = Sched(nc, es)
    dt = nc.dram_tensor

    class Lazy:
        def __init__(self, name, shape):
            self.name, self.shape, self.h = name, shape, None

        def ap(self):
            if self.h is None:
                self.h = dt(self.name, self.shape, F32, kind="ExternalInput").ap()
                used_inputs.append(self.name)
            return self.h

        def __getitem__(self, idx):
            return self.ap()[idx]

    used_inputs = []
    nc._used_inputs = used_inputs
    x_in = Lazy("x_in", [128, NB * T]).ap()
    y_out = dt("y_out", [128, NB * T], F32, kind="ExternalOutput").ap()
    consts_d = Lazy("consts", [128, NCONST]).ap()
    wg_d = [Lazy("wg%d" % l, [FB, 128, NB * 128]) for l in range(DEPTH)]
    wu_d = [Lazy("wu%d" % l, [FB, 128, NB * 128]) for l in range(DEPTH)]
    wd_d = [Lazy("wd%d" % l, [NB, 128, FB * 128]) for l in range(DEPTH)]
    cwin_d = Lazy("cwin", [2, 24, 128, NB * 128])
    cwout_d = Lazy("cwout", [2, NB, 128, NB * 128])
    pw_d = Lazy("pw", [128, 4 * 2 * 256])
    aqk_d = Lazy("aqk", [16, 128, NB * 128])
    av_d = Lazy("av", [128, NB * 1024])
    ao_d = Lazy("ao", [NB, 128, NB * 128])
    lamv_d = Lazy("lamv", [128, 512])
    strips_d = Lazy("strips", [4, 128, 2 * 1152])

    edge_in = dt("edge_in", [128, 16], F32)
    edge_out = dt("edge_out", [2 * 128, 16], F32)
    pedge_in = dt("pedge_in", [128, 128], F32)
    pedge_out = dt("pedge_out", [2 * 128, 128], F32)
    kt_in = [dt("kt_in%d" % i, [4 * 128, T], BF16) for i in range(2)]
    kt_all = [dt("kt_all%d" % i, [2 * 4 * 128, T], BF16) for i in range(2)]
    v_in = [dt("v_in%d" % i, [8 * 128, D], BF16) for i in range(2)]
    v_all = [dt("v_all%d" % i, [2 * 8 * 128, D], BF16) for i in range(2)]

    sb = lambda name, shape, dtype: es.enter_context(nc.sbuf_tensor(name, shape, dtype))
    X = sb("X", [128, NB * T], F32)
    HN = sb("HN", [128, NB * T], BF16)
    BIG = sb("BIG", [128, FB * 1024], BF16)
    M = sb("M", [128, NB * 1024], F32)
    W8 = sb("W8", [128, 6 * 1024], BF16)
    W22 = sb("W22", [128, 2 * FB * 128], BF16)
    RSTD = sb("RSTD", [128, 2 * CH], F32)
    SQ = sb("SQ", [128, 2 * CH], BF16)
    TMP = sb("TMP", [128, 2 * CH], F32)
    CONST = sb("CONST", [128, NCONST], F32)
    ONES = sb("ONES", [128, 128], BF16)
    SMALL = sb("SMALL", [128, 160], F32)
    PS = [es.enter_context(nc.psum_tensor("ps%d" % i, [128, CH], F32)) for i in range(8)]

    r_X = [S.res("X%d" % c) for c in range(NCH)]
    r_HN = [S.res("HN%d" % c) for c in range(NCH)]
    r_PS = [S.res("ps%d" % i) for i in range(8)]
    r_W8 = [S.res("w8_%d" % i) for i in range(6)]
    r_W22 = [S.res("w22_%d" % i) for i in range(2)]
    r_RSTD = [S.res("rstd%d" % i) for i in range(2)]
    r_SQ = [S.res("sq%d" % i) for i in range(2)]
    r_TMP = [S.res("tmp%d" % i) for i in range(2)]
    r_M = [S.res("M%d" % i) for i in range(2)]
    r_CONST = S.res("const")
    r_ONES = S.res("ones")
    r_SMALL = S.res("small")
    r_BIG = S.res("big")
    r_edge_in = S.res("edge_in")
    r_edge_out = S.res("edge_out")

    def xs(k, c, n=CH):
        o = k * T + c * CH
        return X[:, o:o + n]

    def hs(k, c):
        o = k * T + c * CH
        return HN[:, o:o + CH]

    def w8s(slot, k):
        o = slot * 1024 + k * 128
        return W8[:, o:o + 128]

    def w22s(slot, f):
        o = slot * FB * 128 + f * 128
        return W22[:, o:o + 128]

    def cc(col, n=1):
        return CONST[:, col:col + n]

    rot = {"sq": 0, "tmp": 0, "rstd": 0, "w8": 0, "w22": 0}

    def nxt(name, n):
        v = rot[name]
        rot[name] = (v + 1) % n
        return v

    S.op("sp", lambda e: e.dma_start(out=CONST[:], in_=consts_d[:, :]), writes=[r_CONST], dma=True)
    for c in range(NCH):
        for k in range(NB):
            pass
    for c in range(NCH):
        S.op("sp", lambda e, c=c: e.dma_start(
            out=X[:].rearrange("p (k t) -> p k t", k=NB)[:, :, c * CH:(c + 1) * CH],
            in_=x_in.rearrange("p (k t) -> p k t", k=NB)[:, :, c * CH:(c + 1) * CH]),
            writes=[r_X[c]], dma=True)
    S.op("dve", lambda e: e.memset(ONES[:], 1.0), writes=[r_ONES])
    S.op("dve", lambda e: e.tensor_scalar(out=cc(C_G, 128), in0=cc(C_G, 128), scalar1=32.0, scalar2=None,
                                         op0=ALU.mult), reads=[r_CONST], writes=[r_CONST])

    def gcol(l, j, k):
        return cc(C_G + (l * 4 + j) * 8 + k)

    def rstd_from_ps(ps_i, r, dim_eps, out_ap=None, out_res=None):
        o = RSTD[:, r * CH:(r + 1) * CH] if out_ap is None else out_ap
        ores = r_RSTD[r] if out_res is None else out_res
        S.op("act", lambda e: e.activation(out=o, in_=PS[ps_i][:], func=AF.Sqrt,
                                           bias=cc(C_EPS + (0 if dim_eps > 5e-4 else 1)), scale=1.0),
             reads=[r_PS[ps_i], r_CONST], writes=[ores])
        S.op("dve", lambda e: e.reciprocal(out=o, in_=o), reads=[ores], writes=[ores])

    def sumsq_rstd(src_fn, src_res, nblk, ps_i, dimscale_eps, out_ap=None, out_res=None):
        for k in range(nblk):
            s = nxt("sq", 2)
            S.op("act", lambda e, k=k, s=s: e.activation(out=SQ[:, s * CH:(s + 1) * CH], in_=src_fn(k), func=AF.Square),
                 reads=[src_res(k)], writes=[r_SQ[s]])
            if SUB >= 2:
                S.op("pe", lambda e, k=k, s=s: e.matmul(PS[ps_i][:], ONES[:], SQ[:, s * CH:(s + 1) * CH],
                                                       start=(k == 0), stop=(k == nblk - 1)),
                     reads=[r_SQ[s], r_ONES], writes=[r_PS[ps_i]])
        r = nxt("rstd", 2)
        if SUB >= 3:
            rstd_from_ps(ps_i, r, dimscale_eps, out_ap, out_res)
        return r

    def prenorm(l, j, chunks=range(NCH)):
        for c in chunks:
            r = sumsq_rstd(lambda k, c=c: xs(k, c), lambda k, c=c: r_X[c], NB, 6 + (c % 2), D * EPS)
            for k in range(NB if SUB >= 5 else 0):
                t = nxt("tmp", 2)
                S.op("dve", lambda e, k=k, c=c, r=r, t=t: e.tensor_tensor(
                    out=TMP[:, t * CH:(t + 1) * CH], in0=xs(k, c), in1=RSTD[:, r * CH:(r + 1) * CH], op=ALU.mult),
                    reads=[r_X[c], r_RSTD[r]], writes=[r_TMP[t]])
                S.op("act", lambda e, k=k, c=c, t=t: e.activation(
                    out=hs(k, c), in_=TMP[:, t * CH:(t + 1) * CH], func=AF.Copy, scale=gcol(l, j, k)),
                    reads=[r_TMP[t], r_CONST], writes=[r_HN[c]])

    def load_w8(slot, src_ap):
        S.op("pool", lambda e: e.dma_start(out=W8[:, slot * 1024:(slot + 1) * 1024], in_=src_ap),
             writes=[r_W8[slot]], dma=True)

    def load_w22(slot, src_ap):
        S.op("pool", lambda e: e.dma_start(out=W22[:, slot * FB * 128:(slot + 1) * FB * 128], in_=src_ap),
             writes=[r_W22[slot]], dma=True)

    def mm_group(ps_i, terms, reads):
        n = len(terms)

        def fn(e):
            ins = None
            for i, (a, b) in enumerate(terms):
                ins = e.matmul(PS[ps_i][:], a, b, start=(i == 0), stop=(i == n - 1))
            return ins
        S.op("pe", fn, reads=reads, writes=[r_PS[ps_i]])

    def ms(d, ci):
        o = d * 1024 + ci * CH
        return M[:, o:o + CH]

    def proj_post(l, j, chunk_pairs, wload, nk, lhs_fn, rhs_fn, rhs_res, slot_res, evac_scale=None, sq_scale=None, rhs_fn_d=None):
        if evac_scale is None:
            evac_scale = lambda d: gcol(l, j, d)
        pending = []
        for pair in chunk_pairs:
            for d in range(NB):
                slot = wload(d)
                for ci, c in enumerate(pair):
                    pi = 4 + ci
                    mm_group(pi, [(lhs_fn(slot, d, k), rhs_fn(k, c) if rhs_fn_d is None else rhs_fn_d(d, k, c)) for k in range(nk)],
                             reads=[slot_res(slot)] + rhs_res(c))
                    S.op("act", lambda e, d=d, ci=ci, pi=pi: e.activation(out=ms(d, ci), in_=PS[pi][:], func=AF.Copy,
                                                                       scale=evac_scale(d)),
                         reads=[r_PS[pi], r_CONST, r_SMALL], writes=[r_M[ci]])
                    s = nxt("sq", 2)
                    if sq_scale is None:
                        S.op("act", lambda e, pi=pi, s=s: e.activation(out=SQ[:, s * CH:(s + 1) * CH], in_=PS[pi][:],
                                                                    func=AF.Square),
                             reads=[r_PS[pi]], writes=[r_SQ[s]])
                    else:
                        S.op("act", lambda e, pi=pi, s=s, d=d: e.activation(out=SQ[:, s * CH:(s + 1) * CH], in_=PS[pi][:],
                                                                         func=AF.Square, scale=sq_scale(d)),
                             reads=[r_PS[pi], r_CONST], writes=[r_SQ[s]])
                    for fnp in pending:
                        fnp()
                    del pending[:]
                    pending.append(lambda d=d, ci=ci, s=s: S.op(
                        "pe", lambda e: e.matmul(PS[6 + ci][:], ONES[:], SQ[:, s * CH:(s + 1) * CH],
                                                 start=(d == 0), stop=(d == NB - 1)),
                        reads=[r_SQ[s], r_ONES], writes=[r_PS[6 + ci]]))
            for fnp in pending:
                fnp()
            del pending[:]
            for ci, c in enumerate(pair):
                r = nxt("rstd", 2)
                rstd_from_ps(6 + ci, r, D * EPS)
                for d in range(NB):
                    t = nxt("tmp", 2)
                    S.op("dve", lambda e, d=d, ci=ci, r=r, t=t: e.tensor_tensor(
                        out=TMP[:, t * CH:(t + 1) * CH], in0=ms(d, ci), in1=RSTD[:, r * CH:(r + 1) * CH], op=ALU.mult),
                        reads=[r_M[ci], r_RSTD[r]], writes=[r_TMP[t]])
                    S.op("dve", lambda e, d=d, c=c, t=t: e.tensor_tensor(
                        out=xs(d, c), in0=xs(d, c), in1=TMP[:, t * CH:(t + 1) * CH], op=ALU.add),
                        reads=[r_TMP[t], r_X[c]], writes=[r_X[c]])

    def w8_loader(src_fn):
        def wload(d):
            slot = nxt("w8", 6)
            load_w8(slot, src_fn(d))
            return slot
        return wload

    def a_s(f, ci):
        o = f * 1024 + ci * CH
        return BIG[:, o:o + CH]

    r_A = [[S.res("a%d_%d" % (f, ci)) for ci in range(2)] for f in range(FB)]

    def ffn(l):
        for hf in range(2):
            pair = (2 * hf, 2 * hf + 1)
            prenorm(l, 2, pair)
            for f in range(FB):
                sg_ = nxt("w8", 6)
                load_w8(sg_, wg_d[l][f])
                su_ = nxt("w8", 6)
                load_w8(su_, wu_d[l][f])
                for ci, c in enumerate(pair):
                    pg, pu = ci, 2 + ci
                    mm_group(pg, [(w8s(sg_, k), hs(k, c)) for k in range(NB)], reads=[r_W8[sg_], r_HN[c]])
                    mm_group(pu, [(w8s(su_, k), hs(k, c)) for k in range(NB)], reads=[r_W8[su_], r_HN[c]])
                    t = nxt("tmp", 2)
                    S.op("act", lambda e, pg=pg, t=t: e.activation(out=TMP[:, t * CH:(t + 1) * CH], in_=PS[pg][:], func=AF.Silu),
                         reads=[r_PS[pg]], writes=[r_TMP[t]])
                    S.op("dve", lambda e, f=f, ci=ci, pu=pu, t=t: e.tensor_tensor(
                        out=a_s(f, ci), in0=TMP[:, t * CH:(t + 1) * CH], in1=PS[pu][:], op=ALU.mult),
                        reads=[r_TMP[t], r_PS[pu]], writes=[r_A[f][ci]])

            def wload(d):
                slot = nxt("w22", 2)
                load_w22(slot, wd_d[l][d])
                return slot
            proj_post(l, 3, [pair], wload, FB,
                      lambda slot, d, k: w22s(slot, k),
                      lambda k, c, hf=hf: a_s(k, c - 2 * hf),
                      lambda c, hf=hf: [r_A[f][c - 2 * hf] for f in range(FB)],
                      lambda slot: r_W22[slot])

    r_fs = S.res("fence_scratch")

    def fence(reads, writes):
        S.op("dve", lambda e: e.memset(SMALL[:, 159:160], 0.0), reads=list(reads), writes=list(writes) + [r_fs])

    all_A = [r_A[f][ci] for f in range(FB) for ci in range(2)]

    def conv_mixer(l, ia):
        if STAGE < 2:
            return
        prenorm(l, 0)
        if STAGE < 3:
            return
        def U(s, a, n):
            o = s * T + a
            return M[:, o:o + n]

        def Bf(s, a, n):
            o = 2 * T + s * T + a
            return M[:, o:o + n]
        TT = BIG[:, 8 * T: 8 * T + 2 * T].bitcast(F32)
        r_U = [S.res("U0"), S.res("U1")]
        r_B = [S.res("B0"), S.res("B1")]
        T2 = BIG[:, 8 * T + 2 * T: 8 * T + 2 * T + 2048].bitcast(F32)
        r_TT = S.res("TT")
        r_T2 = S.res("T2")
        r_V = S.res("V")
        fence(all_A + r_M, r_U + r_B + [r_TT, r_T2, r_V])

        def cw(tap, f):
            return cc(C_CW + (ia * 3 + tap) * 8 + f)
        for f in range(NB):
            s3 = [nxt("w8", 6) for _ in range(3)]
            for i, sl in enumerate(s3):
                load_w8(sl, cwin_d[ia, i * 8 + f])
            us = f % 2
            for c in range(NCH):
                pb, pc, ph = (0, 1, 2) if c % 2 == 0 else (3, 4, 5)
                for pi, sl in zip((pb, pc, ph), s3):
                    mm_group(pi, [(w8s(sl, k), hs(k, c)) for k in range(NB)], reads=[r_W8[sl], r_HN[c]])
                t = nxt("tmp", 2)
                S.op("act", lambda e, ph=ph, t=t: e.activation(out=TMP[:, t * CH:(t + 1) * CH], in_=PS[ph][:], func=AF.Copy),
                     reads=[r_PS[ph]], writes=[r_TMP[t]])
                S.op("dve", lambda e, pc=pc, t=t, us=us, c=c: e.tensor_tensor(
                    out=U(us, c * CH, CH), in0=TMP[:, t * CH:(t + 1) * CH], in1=PS[pc][:], op=ALU.mult),
                    reads=[r_TMP[t], r_PS[pc]], writes=[r_U[us]])
                S.op("act", lambda e, pb=pb, us=us, c=c: e.activation(out=Bf(us, c * CH, CH), in_=PS[pb][:], func=AF.Copy),
                     reads=[r_PS[pb]], writes=[r_B[us]])
            S.op("dve", lambda e, us=us, f=f: e.tensor_scalar(out=TT, in0=U(us, 0, T), scalar1=cw(1, f), scalar2=None,
                                                           op0=ALU.mult),
                 reads=[r_U[us], r_CONST], writes=[r_TT])
            for tap, (oa, ob, ia_) in ((0, (1, 1024, 0)), (0, (1024, T, 1023)), (2, (0, 1024, 1)), (2, (1024, T - 1, 1025))):
                n = ob - oa
                S.op("act", lambda e, us=us, f=f, tap=tap, ia_=ia_, n=n: e.activation(
                    out=T2[:, 0:n], in_=U(us, ia_, n), func=AF.Copy, scale=cw(tap, f)),
                    reads=[r_U[us], r_CONST], writes=[r_T2])
                S.op("dve", lambda e, oa=oa, ob=ob, n=n: e.tensor_tensor(out=TT[:, oa:ob], in0=TT[:, oa:ob], in1=T2[:, 0:n],
                                                                       op=ALU.add),
                     reads=[r_T2, r_TT], writes=[r_TT])
            S.op("dve", lambda e, us=us, f=f: e.tensor_tensor(out=BIG[:, f * T:(f + 1) * T], in0=TT, in1=Bf(us, 0, T),
                                                            op=ALU.mult),
                 reads=[r_TT, r_B[us]], writes=[r_V])
            for j, tcol in enumerate((0, T - 1)):
                S.op("dve", lambda e, us=us, f=f, j=j, tcol=tcol: e.tensor_copy(
                    out=SMALL[:, 2 * f + j:2 * f + j + 1], in_=U(us, tcol, 1)), reads=[r_U[us]], writes=[r_SMALL])
                S.op("dve", lambda e, f=f, j=j, tcol=tcol: e.tensor_copy(
                    out=SMALL[:, 16 + 2 * f + j:16 + 2 * f + j + 1], in_=TT[:, tcol:tcol + 1]), reads=[r_TT], writes=[r_SMALL])
                S.op("dve", lambda e, us=us, f=f, j=j, tcol=tcol: e.tensor_copy(
                    out=SMALL[:, 32 + 2 * f + j:32 + 2 * f + j + 1], in_=Bf(us, tcol, 1)), reads=[r_B[us]], writes=[r_SMALL])
        if STAGE < 4:
            return
        S.op("sp", lambda e: e.dma_start(out=edge_in[:, 0:16], in_=SMALL[:, 0:16]), reads=[r_SMALL], writes=[r_edge_in], dma=True)
        halo_exchange()
        S.op("sp", lambda e: e.dma_start(out=SMALL[:, 48:80].rearrange("p (r n) -> p r n", r=2),
                                         in_=edge_out.ap().rearrange("(r p) n -> p r n", p=128)[:, :, 0:16]),
             reads=[r_edge_out], writes=[r_SMALL], dma=True)
        sm3 = lambda o: SMALL[:, o:o + 16].rearrange("p (f two) -> p f two", two=2)
        cw3 = lambda tap: CONST[:, C_CW + (ia * 3 + tap) * 8: C_CW + (ia * 3 + tap) * 8 + 8]
        for side, (hoff, hidx, mcol, tap, tcol) in enumerate(((48, 1, C_ML, 0, 0), (64, 0, C_MR, 2, T - 1))):
            hl = SMALL[:, 80 + 8 * side: 88 + 8 * side]
            S.op("dve", lambda e, hl=hl, hoff=hoff, hidx=hidx, mcol=mcol: e.tensor_scalar(
                out=hl, in0=sm3(hoff)[:, :, hidx], scalar1=cc(mcol), scalar2=None, op0=ALU.mult),
                reads=[r_SMALL, r_CONST], writes=[r_SMALL])
            S.op("dve", lambda e, hl=hl, tap=tap: e.tensor_tensor(out=hl, in0=hl, in1=cw3(tap), op=ALU.mult),
                 reads=[r_SMALL, r_CONST], writes=[r_SMALL])
            S.op("dve", lambda e, hl=hl, side=side: e.tensor_tensor(out=hl, in0=hl, in1=sm3(16)[:, :, side], op=ALU.add),
                 reads=[r_SMALL], writes=[r_SMALL])
            S.op("dve", lambda e, hl=hl, side=side, tcol=tcol: e.tensor_tensor(
                out=BIG[:, 0:NB * T].rearrange("p (f t) -> p f t", f=NB)[:, :, tcol],
                in0=hl, in1=sm3(32)[:, :, side], op=ALU.mult),
                reads=[r_SMALL, r_V], writes=[r_V])
        fence(r_U + r_B + [r_TT, r_T2], r_M)
        proj_post(l, 1, [(1, 2), (0, 3)], w8_loader(lambda d: cwout_d[ia, d]), NB,
                  lambda slot, d, k: w8s(slot, k),
                  lambda k, c: BIG[:, k * T + c * CH: k * T + (c + 1) * CH],
                  lambda c: [r_V],
                  lambda slot: r_W8[slot])
        fence([r_V, r_TT, r_T2], all_A)


    def pool_mixer(l):
        HW = T + 16
        RS_ALL = BIG[:, 0:2 * T].bitcast(F32)
        PEDGE = M[:, 0:128]
        PHALO = M[:, 128:384]
        HP = M[:, 384:384 + HW]
        SA = M[:, 384 + HW:384 + 2 * HW]
        SB = M[:, 384 + 2 * HW:384 + 3 * HW]
        r_RS, r_PE, r_PH, r_HP, r_SA, r_SB = (S.res(n) for n in ("RSALL", "PEDGE", "PHALO", "HP", "SA", "SB"))
        r_pin, r_pout, r_PW = S.res("pedge_in"), S.res("pedge_out"), S.res("PW")
        fence(all_A + r_M, [r_RS, r_PE, r_PH, r_HP, r_SA, r_SB])
        PW = W22[:, 0:2048]
        S.op("pool", lambda e: e.dma_start(out=PW, in_=pw_d.ap()), writes=[r_PW] + r_W22, dma=True)
        S.op("dve", lambda e: e.tensor_tensor(out=SMALL[:, 100:108], in0=cc(C_PS, 8), in1=cc(C_G + (l * 4 + 1) * 8, 8), op=ALU.mult),
             reads=[r_CONST, r_SMALL], writes=[r_SMALL])
        for c in range(NCH):
            sumsq_rstd(lambda k, c=c: xs(k, c), lambda k, c=c: r_X[c], NB, 6 + (c % 2), D * EPS,
                       out_ap=RS_ALL[:, c * CH:(c + 1) * CH], out_res=r_RS)
        for f in range(NB):
            for side, a in enumerate((0, T - 8)):
                o = f * 16 + side * 8
                S.op("dve", lambda e, f=f, a=a, o=o: e.tensor_tensor(out=PEDGE[:, o:o + 8], in0=X[:, f * T + a:f * T + a + 8],
                                                                  in1=RS_ALL[:, a:a + 8], op=ALU.mult),
                     reads=[r_X[0], r_X[3], r_RS], writes=[r_PE])
                S.op("dve", lambda e, f=f, o=o: e.tensor_scalar(out=PEDGE[:, o:o + 8], in0=PEDGE[:, o:o + 8],
                                                             scalar1=gcol(l, 0, f), scalar2=None, op0=ALU.mult),
                     reads=[r_PE, r_CONST], writes=[r_PE])
        S.op("sp", lambda e: e.dma_start(out=pedge_in[:, :], in_=PEDGE), reads=[r_PE], writes=[r_pin], dma=True)
        S.op("pool", lambda e: e.collective_compute("AllGather", ALU.bypass, replica_groups=PAIRS,
                                                    ins=[pedge_in.ap().opt()], outs=[pedge_out.ap().opt()]),
             reads=[r_pin], writes=[r_pout])
        S.op("sp", lambda e: e.dma_start(out=PHALO.rearrange("p (r n) -> p r n", r=2),
                                         in_=pedge_out.ap().rearrange("(r p) n -> p r n", p=128)),
             reads=[r_pout], writes=[r_PH], dma=True)
        for f in range(NB):
            g = f // 2
            w = POOLW[g]
            S.op("dve", lambda e, f=f: e.tensor_scalar(out=HP[:, 0:8], in0=PHALO[:, f * 16 + 8:f * 16 + 16], scalar1=cc(C_ML),
                                                     scalar2=None, op0=ALU.mult), reads=[r_PH, r_CONST], writes=[r_HP])
            S.op("dve", lambda e, f=f: e.tensor_scalar(out=HP[:, 8 + T:16 + T], in0=PHALO[:, 128 + f * 16:128 + f * 16 + 8],
                                                     scalar1=cc(C_MR), scalar2=None, op0=ALU.mult),
                 reads=[r_PH, r_CONST], writes=[r_HP])
            S.op("dve", lambda e, f=f: e.tensor_tensor(out=HP[:, 8:8 + T], in0=X[:, f * T:(f + 1) * T], in1=RS_ALL, op=ALU.mult),
                 reads=r_X + [r_RS], writes=[r_HP])
            S.op("dve", lambda e, f=f: e.tensor_scalar(out=HP[:, 8:8 + T], in0=HP[:, 8:8 + T], scalar1=gcol(l, 0, f),
                                                     scalar2=None, op0=ALU.mult), reads=[r_HP, r_CONST], writes=[r_HP])
            S.op("dve", lambda e: e.tensor_tensor(out=SA[:, 1:HW], in0=HP[:, 0:HW - 1], in1=HP[:, 1:HW], op=ALU.add),
                 reads=[r_HP], writes=[r_SA])
            cur, cur_r, oth, oth_r = SA, r_SA, SB, r_SB
            if w >= 4:
                S.op("dve", lambda e: e.tensor_tensor(out=SB[:, 2:HW - 1], in0=SA[:, 1:HW - 2], in1=SA[:, 3:HW], op=ALU.add),
                     reads=[r_SA], writes=[r_SB])
                cur, cur_r, oth, oth_r = SB, r_SB, SA, r_SA
            if w >= 8:
                S.op("dve", lambda e: e.tensor_tensor(out=SA[:, 4:HW - 3], in0=SB[:, 2:HW - 5], in1=SB[:, 6:HW - 1], op=ALU.add),
                     reads=[r_SB], writes=[r_SA])
                cur, cur_r, oth, oth_r = SA, r_SA, SB, r_SB
            if w >= 16:
                S.op("dve", lambda e: e.tensor_tensor(out=SB[:, 8:HW - 7], in0=SA[:, 4:HW - 11], in1=SA[:, 12:HW - 3], op=ALU.add),
                     reads=[r_SA], writes=[r_SB])
                cur, cur_r, oth, oth_r = SB, r_SB, SA, r_SA
            S.op("dve", lambda e, cur=cur, g=g: e.tensor_tensor(out=cur[:, 8:16], in0=cur[:, 8:16], in1=cc(C_CL + g * 8, 8), op=ALU.mult),
                 reads=[cur_r, r_CONST], writes=[cur_r])
            S.op("dve", lambda e, cur=cur, g=g: e.tensor_tensor(out=cur[:, T:T + 8], in0=cur[:, T:T + 8], in1=cc(C_CR + g * 8, 8), op=ALU.mult),
                 reads=[cur_r, r_CONST], writes=[cur_r])
            S.op("act", lambda e, cur=cur, oth=oth, w=w: e.activation(out=oth[:, 8:8 + T], in_=cur[:, 8:8 + T], func=AF.Copy,
                                                                    scale=1.0 / w), reads=[cur_r], writes=[oth_r])
            S.op("dve", lambda e, oth=oth, f=f: e.tensor_tensor(out=HN[:, f * T:(f + 1) * T], in0=oth[:, 8:8 + T], in1=HP[:, 8:8 + T],
                                                              op=ALU.subtract), reads=[oth_r, r_HP], writes=r_HN)
        fence([r_PE, r_PH, r_HP, r_SA, r_SB], r_M)

        def lhs(slot, d, k):
            g, dblk = d // 2, d % 2
            o = (g * 2 + k) * 256 + dblk * 128
            return PW[:, o:o + 128]
        proj_post(l, 1, [(0, 1), (2, 3)], lambda d: 0, 2, lhs,
                  lambda k, c: None, lambda c: [r_HN[c]], lambda slot: r_PW,
                  evac_scale=lambda d: SMALL[:, 100 + d:101 + d], sq_scale=lambda d: cc(C_PS + d),
                  rhs_fn_d=lambda d, k, c: hs((d // 2) * 2 + k, c))
        fence([r_RS, r_PW], all_A + r_W22)

    def attn_mixer(l):
        li = lambda_init(l)
        MB = M[:].bitcast(BF16)
        WF = W22[:].bitcast(F32)
        r_lam = S.res("lamtmp")
        fence(r_TMP, [r_lam])
        S.op("sp", lambda e: e.dma_start(out=TMP[:, 0:512], in_=lamv_d.ap()), writes=[r_lam, r_TMP[0]], dma=True)
        for i in range(2):
            S.op("dve", lambda e, i=i: e.tensor_tensor(out=TMP[:, 512 + 128 * i:640 + 128 * i], in0=TMP[:, 256 * i:256 * i + 128],
                                                     in1=TMP[:, 256 * i + 128:256 * i + 256], op=ALU.mult),
                 reads=[r_lam], writes=[r_TMP[1]])
            S.op("dve", lambda e, i=i: e.reduce_sum(out=SMALL[:, 110 + i:111 + i], in_=TMP[:, 512 + 128 * i:640 + 128 * i],
                                                  axis=mybir.AxisListType.X), reads=[r_TMP[1]], writes=[r_SMALL])
        S.op("act", lambda e: e.activation(out=SMALL[:, 112:114], in_=SMALL[:, 110:112], func=AF.Exp), reads=[r_SMALL], writes=[r_SMALL])
        S.op("dve", lambda e: e.tensor_tensor(out=SMALL[:, 114:115], in0=SMALL[:, 112:113], in1=SMALL[:, 113:114], op=ALU.subtract),
             reads=[r_SMALL], writes=[r_SMALL])
        S.op("dve", lambda e: e.tensor_scalar(out=SMALL[:, 115:116], in0=SMALL[:, 114:115], scalar1=li, scalar2=-1.0,
                                            op0=ALU.add, op1=ALU.mult), reads=[r_SMALL], writes=[r_SMALL])
        S.op("dve", lambda e: e.tensor_scalar(out=SMALL[:, 116:118], in0=cc(C_SG, 2), scalar1=16.0 * (1.0 - li), scalar2=None,
                                            op0=ALU.mult), reads=[r_SMALL, r_CONST], writes=[r_SMALL])
        fence([r_lam], r_TMP)
        neglam = SMALL[:, 115:116]
        prenorm(l, 0)
        r_Q = S.res("QT")
        r_KST = [S.res("kst0"), S.res("kst1")]
        r_VST = [S.res("vst0"), S.res("vst1")]
        r_WV = S.res("WV")
        r_ktin = [S.res("kt_in0"), S.res("kt_in1")]
        r_ktall = [S.res("kt_all0"), S.res("kt_all1")]
        r_vin = [S.res("v_in0"), S.res("v_in1")]
        r_vall = [S.res("v_all0"), S.res("v_all1")]

        def gather(src, dst, r_src, r_dst):
            S.op("pool", lambda e: e.collective_compute("AllGather", ALU.bypass, replica_groups=PAIRS,
                                                        ins=[src.ap().opt()], outs=[dst.ap().opt()]),
                 reads=[r_src], writes=[r_dst])
        fence(all_A + r_M, [r_Q, r_WV] + r_KST + r_VST)
        WV = MB[:, 4096:12288]
        for i in range(2):
            S.op("pool", lambda e, i=i: e.dma_start(out=WV[:, i * 4096:(i + 1) * 4096], in_=av_d.ap()[:, i * 4096:(i + 1) * 4096]),
                 writes=[r_WV], dma=True)
        qscale = 128 ** -0.5
        bank = [0]

        def nb():
            bank[0] = (bank[0] + 1) % 4
            return bank[0]
        for j in range(NB):
            sl = nxt("w8", 6)
            load_w8(sl, aqk_d[j])
            for c in range(NCH):
                pi = nb()
                mm_group(pi, [(w8s(sl, k), hs(k, c)) for k in range(NB)], reads=[r_W8[sl], r_HN[c]])
                S.op("act", lambda e, pi=pi, j=j, c=c: e.activation(out=BIG[:, j * T + c * CH:j * T + (c + 1) * CH], in_=PS[pi][:],
                                                                  func=AF.Copy, scale=qscale), reads=[r_PS[pi]], writes=[r_Q])
        for j in range(NB):
            sl = nxt("w8", 6)
            load_w8(sl, aqk_d[8 + j])
            ks = j % 2
            for c in range(NCH):
                pi = nb()
                mm_group(pi, [(w8s(sl, k), hs(k, c)) for k in range(NB)], reads=[r_W8[sl], r_HN[c]])
                S.op("act", lambda e, pi=pi, ks=ks, c=c: e.activation(out=MB[:, ks * T + c * CH:ks * T + (c + 1) * CH], in_=PS[pi][:],
                                                                   func=AF.Copy), reads=[r_PS[pi]], writes=[r_KST[ks]])
            S.op("sp", lambda e, j=j, ks=ks: e.dma_start(out=kt_in[j // 4][(j % 4) * 128:(j % 4 + 1) * 128, :],
                                                       in_=MB[:, ks * T:(ks + 1) * T]),
                 reads=[r_KST[ks]], writes=[r_ktin[j // 4]], dma=True)
            if j % 4 == 3:
                gather(kt_in[j // 4], kt_all[j // 4], r_ktin[j // 4], r_ktall[j // 4])
        for tb in range(16):
            vs = tb % 2
            for ec in range(2):
                pi = nb()
                mm_group(pi, [(HN[:, k * T + tb * 128:k * T + (tb + 1) * 128], WV[:, k * 1024 + ec * 512:k * 1024 + (ec + 1) * 512])
                              for k in range(NB)], reads=[r_WV] + r_HN)
                S.op("act", lambda e, pi=pi, vs=vs, ec=ec: e.activation(
                    out=MB[:, 12288 + vs * 1024 + ec * 512:12288 + vs * 1024 + (ec + 1) * 512], in_=PS[pi][:], func=AF.Copy),
                    reads=[r_PS[pi]], writes=[r_VST[vs]])
            S.op("sp", lambda e, tb=tb, vs=vs: e.dma_start(out=v_in[tb // 8][(tb % 8) * 128:(tb % 8 + 1) * 128, :],
                                                         in_=MB[:, 12288 + vs * 1024:12288 + (vs + 1) * 1024]),
                 reads=[r_VST[vs]], writes=[r_vin[tb // 8]], dma=True)
            if tb % 8 == 7:
                gather(v_in[tb // 8], v_all[tb // 8], r_vin[tb // 8], r_vall[tb // 8])
        r_KT = [S.res("KT0"), S.res("KT1")]
        r_V = S.res("Vh")
        r_ST = S.res("strips")
        r_ON0, r_OD = S.res("On0"), S.res("Od")
        r_P = [S.res("P%d" % i) for i in range(3)]
        fence(r_KST + r_VST + [r_WV] + r_W22, r_KT + [r_V, r_ON0, r_OD] + r_P)
        fence(all_A, [r_ST])
        STRIP = BIG[:, 8 * T:8 * T + 2 * 2304].bitcast(F32)
        KT = lambda t: MB[:, t * 4096:(t + 1) * 4096]
        VH = MB[:, 8192:16384]
        ON0 = lambda eb: WF[:, eb * CH:(eb + 1) * CH]
        OD = lambda eb: WF[:, 1024 + eb * CH:1024 + (eb + 1) * CH]
        PT = lambda i: WF[:, 2048 + 256 * i:2048 + 256 * (i + 1)].bitcast(BF16)
        SB_ = [0, 1, 6, 7]
        LOOK = 2
        items = [(h, qc, t) for h in range(4) for qc in range(NCH) for t in range(2)]
        steps = [(it, kb) for it in range(len(items)) for kb in range(32)]
        NS = len(steps)
        deferred = []

        def head_loads(h):
            S.op("sp", lambda e, h=h: e.dma_start(out=STRIP, in_=strips_d[h]), writes=[r_ST], dma=True)
            for r in range(2):
                for hv in range(2):
                    b0 = r * 16 + hv * 8
                    S.op("sp", lambda e, h=h, r=r, hv=hv, b0=b0: e.dma_start(
                        out=VH[:, b0 * 256:(b0 + 8) * 256].rearrange("p (b e) -> p b e", e=256),
                        in_=v_all[hv].ap().rearrange("(b p) e -> p b e", p=128)[:, r * 8:(r + 1) * 8, h * 256:(h + 1) * 256]),
                        reads=[r_vall[hv]], writes=[r_V], dma=True)
            for t in range(2):
                j = 2 * h + t
                for r in range(2):
                    S.op("sp", lambda e, t=t, j=j, r=r: e.dma_start(
                        out=KT(t)[:, r * T:(r + 1) * T],
                        in_=kt_all[j // 4][r * 512 + (j % 4) * 128:r * 512 + (j % 4 + 1) * 128, :]),
                        reads=[r_ktall[j // 4]], writes=[r_KT[t]], dma=True)

        def emit_S(si):
            it, kb = steps[si]
            h, qc, t = items[it]
            if kb == 0 and t == 0 and qc == 0:
                head_loads(h)
            j = 2 * h + t
            sbk = SB_[si % 4]
            mm_group(sbk, [(KT(t)[:, kb * 128:(kb + 1) * 128], BIG[:, j * T + qc * CH:j * T + (qc + 1) * CH])],
                     reads=[r_KT[t], r_Q])
            pi = si % 3
            da, dbb = kb - 4 * qc + 1, kb - 16 - 4 * qc + 1
            if 0 <= da <= 5 or 0 <= dbb <= 5:
                off = (5 - da) * 128 if 0 <= da <= 5 else 1152 + (5 - dbb) * 128
                tt = nxt("tmp", 2)
                S.op("dve", lambda e, sbk=sbk, off=off, tt=tt: e.tensor_tensor(
                    out=TMP[:, tt * CH:(tt + 1) * CH], in0=PS[sbk][:], in1=STRIP[:, off:off + CH], op=ALU.add),
                    reads=[r_PS[sbk], r_ST], writes=[r_TMP[tt]])
                S.op("act", lambda e, tt=tt, pi=pi: e.activation(out=PT(pi), in_=TMP[:, tt * CH:(tt + 1) * CH], func=AF.Exp),
                     reads=[r_TMP[tt]], writes=[r_P[pi]])
            else:
                cls = 0 if da < 0 else (2 if dbb > 5 else 1)
                S.op("act", lambda e, sbk=sbk, pi=pi, h=h, cls=cls: e.activation(
                    out=PT(pi), in_=PS[sbk][:], func=AF.Exp, bias=cc(C_BC + h * 3 + cls), scale=1.0),
                    reads=[r_PS[sbk], r_CONST], writes=[r_P[pi]])

        def emit_PV(si):
            it, kb = steps[si]
            h, qc, t = items[it]
            pi = si % 3
            for eb in range(2):
                S.op("pe", lambda e, eb=eb, kb=kb, pi=pi: e.matmul(
                    PS[2 + eb][:], VH[:, kb * 256 + eb * 128:kb * 256 + (eb + 1) * 128], PT(pi),
                    start=(kb == 0), stop=(kb == 31)), reads=[r_V, r_P[pi]], writes=[r_PS[2 + eb]])
            S.op("pe", lambda e, kb=kb, pi=pi: e.matmul(PS[4][:], ONES[:], PT(pi), start=(kb == 0), stop=(kb == 31)),
                 reads=[r_ONES, r_P[pi]], writes=[r_PS[4]])
            if kb == 31:
                item_end(si, h, qc, t)

        def item_end(si, h, qc, t):
            rr = nxt("rstd", 2)
            S.op("dve", lambda e, rr=rr: e.reciprocal(out=RSTD[:, rr * CH:(rr + 1) * CH], in_=PS[4][:]),
                 reads=[r_PS[4]], writes=[r_RSTD[rr]])
            dst, r_dst = (ON0, r_ON0) if t == 0 else (OD, r_OD)
            for eb in range(2):
                S.op("act", lambda e, eb=eb, dst=dst: e.activation(out=dst(eb), in_=PS[2 + eb][:], func=AF.Copy),
                     reads=[r_PS[2 + eb]], writes=[r_dst])
            for eb in range(2):
                S.op("dve", lambda e, eb=eb, rr=rr, dst=dst: e.tensor_tensor(out=dst(eb), in0=dst(eb),
                                                                           in1=RSTD[:, rr * CH:(rr + 1) * CH], op=ALU.mult),
                     reads=[r_dst, r_RSTD[rr]], writes=[r_dst])
                if t == 1:
                    S.op("dve", lambda e, eb=eb: e.tensor_scalar(out=OD(eb), in0=OD(eb), scalar1=neglam, scalar2=None, op0=ALU.mult),
                         reads=[r_OD, r_SMALL], writes=[r_OD])
                    S.op("dve", lambda e, eb=eb: e.tensor_tensor(out=OD(eb), in0=OD(eb), in1=ON0(eb), op=ALU.add),
                         reads=[r_OD, r_ON0], writes=[r_OD])
            if t == 1:
                sqs = []
                for eb in range(2):
                    sl = nxt("sq", 2)
                    sqs.append(sl)
                    S.op("act", lambda e, eb=eb, sl=sl: e.activation(out=SQ[:, sl * CH:(sl + 1) * CH], in_=OD(eb), func=AF.Square),
                         reads=[r_OD], writes=[r_SQ[sl]])

                def later(h=h, qc=qc, sqs=sqs):
                    for eb in range(2):
                        sl = sqs[eb]
                        S.op("pe", lambda e, eb=eb, sl=sl: e.matmul(PS[5][:], ONES[:], SQ[:, sl * CH:(sl + 1) * CH],
                                                                  start=(eb == 0), stop=(eb == 1)),
                             reads=[r_SQ[sl], r_ONES], writes=[r_PS[5]])
                    rr2 = nxt("rstd", 2)
                    rstd_from_ps(5, rr2, 256 * EPS)
                    for eb in range(2):
                        tt = nxt("tmp", 2)
                        S.op("dve", lambda e, eb=eb, rr2=rr2, tt=tt: e.tensor_tensor(
                            out=TMP[:, tt * CH:(tt + 1) * CH], in0=OD(eb), in1=RSTD[:, rr2 * CH:(rr2 + 1) * CH], op=ALU.mult),
                            reads=[r_OD, r_RSTD[rr2]], writes=[r_TMP[tt]])
                        S.op("act", lambda e, eb=eb, tt=tt, h=h, qc=qc: e.activation(
                            out=hs(2 * h + eb, qc), in_=TMP[:, tt * CH:(tt + 1) * CH], func=AF.Copy,
                            scale=SMALL[:, 116 + eb:117 + eb]), reads=[r_TMP[tt], r_SMALL], writes=[r_HN[qc]])
                deferred.append((si + 10, later))

        npv = 0
        for si in range(NS + LOOK):
            if si < NS:
                if si > 0 and si % (NCH * 2 * 32) == 0:
                    while npv < si:
                        emit_PV(npv)
                        npv += 1
                emit_S(si)
            if si - LOOK >= 0 and npv <= si - LOOK:
                emit_PV(npv)
                npv += 1
            while deferred and deferred[0][0] <= npv - 1:
                deferred.pop(0)[1]()
        while npv < NS:
            emit_PV(npv)
            npv += 1
        while deferred:
            deferred.pop(0)[1]()
        fence(r_KT + [r_V, r_ON0, r_OD] + r_P, r_M + r_W22)
        proj_post(l, 1, [(0, 1), (2, 3)], w8_loader(lambda d: ao_d[d]), NB,
                  lambda slot, d, k: w8s(slot, k), lambda k, c: hs(k, c), lambda c: [r_HN[c]], lambda slot: r_W8[slot])
        fence([r_Q, r_ST], all_A)

    r_cc = S.res("cc")

    def halo_exchange():
        S.op("pool", lambda e: e.collective_compute("AllGather", ALU.bypass, replica_groups=PAIRS,
                                                    ins=[edge_in.ap().opt()], outs=[edge_out.ap().opt()]),
             reads=[r_edge_in], writes=[r_edge_out])

    for l in layers:
        kind = l % 3
        if kind == 0:
            conv_mixer(l, l // 3)
        elif kind == 1:
            pool_mixer(l)
        else:
            attn_mixer(l)
        if STAGE >= 5:
            ffn(l)

    r_out = [S.res("out%d" % c) for c in range(NCH)]
    evs = []
    for c in range(NCH):
        evs.append(S.op("sp", lambda e, c=c: e.dma_start(
            out=y_out.rearrange("p (k t) -> p k t", k=NB)[:, :, c * CH:(c + 1) * CH],
            in_=X[:].rearrange("p (k t) -> p k t", k=NB)[:, :, c * CH:(c + 1) * CH]),
            reads=[r_X[c]], writes=[r_out[c]], dma=True))
    S.final_wait("sp", evs)
    S.emit()
    es.close()
    return nc


def _tile_kn(w, kb):
    K, N = w.shape
    return np.ascontiguousarray(w.reshape(kb, 128, N // 128, 128).transpose(2, 1, 0, 3).reshape(N // 128, 128, kb * 128))


def _pcol(v):
    return v.reshape(-1, 128).T


def _t5_bucket_np(rel):
    half = 16
    ret = np.where(rel > 0, half, 0)
    n = np.abs(rel)
    max_exact = 8
    nf = np.maximum(n, 1).astype(np.float32)
    large = max_exact + (np.log(nf / max_exact) / math.log(128 / max_exact) * (half - max_exact)).astype(np.int32)
    large = np.minimum(large, half - 1)
    return ret + np.where(n < max_exact, n, large)


def prep_inputs(inp):
    f32 = np.float32
    x = np.asarray(inp["x"], f32)
    shared = {}
    for l in range(DEPTH):
        shared["wg%d" % l] = _tile_kn(np.asarray(inp["ffn_w_gate"][l], f32), 8)
        shared["wu%d" % l] = _tile_kn(np.asarray(inp["ffn_w_up"][l], f32), 8)
        shared["wd%d" % l] = _tile_kn(np.asarray(inp["ffn_w_down"][l], f32), FB)
    shared["cwin"] = np.stack([_tile_kn(np.asarray(inp["conv_w_in"][i], f32), 8) for i in range(2)])
    shared["cwout"] = np.stack([_tile_kn(np.asarray(inp["conv_w_out"][i], f32), 8) for i in range(2)])
    pw = np.asarray(inp["pool_w"][0], f32)
    shared["pw"] = np.ascontiguousarray(pw.reshape(4, 2, 128, 256).transpose(2, 0, 1, 3).reshape(128, 4 * 2 * 256))
    wqkv = np.asarray(inp["attn_w_qkv"][0], f32)
    shared["aqk"] = _tile_kn(wqkv[:, :2048], 8)
    shared["av"] = np.ascontiguousarray(wqkv[:, 2048:].reshape(8, 128, 1024).transpose(1, 0, 2).reshape(128, 8 * 1024))
    shared["ao"] = _tile_kn(np.asarray(inp["attn_w_o"][0], f32), 8)
    lamv = np.concatenate([np.asarray(inp[k][0], f32) for k in ("lambda_q1", "lambda_k1", "lambda_q2", "lambda_k2")])
    shared["lamv"] = np.ascontiguousarray(np.broadcast_to(lamv[None, :], (128, 512)))
    rel_bias = np.asarray(inp["rel_bias"], f32)
    maps = []
    for c in range(8):
        b, half = c // 2, c % 2
        m = dict(shared)
        xc = x[b, half * T:(half + 1) * T, :]
        m["x_in"] = np.ascontiguousarray(xc.T.reshape(NB, 128, T).transpose(1, 0, 2).reshape(128, NB * T))
        cst = np.zeros((128, NCONST), f32)
        ng = np.asarray(inp["norm_g"], f32)
        for l in range(DEPTH):
            for j in range(4):
                cst[:, C_G + (l * 4 + j) * 8: C_G + (l * 4 + j) * 8 + 8] = _pcol(ng[l, j])
        cwv = np.asarray(inp["conv_w"], f32)
        for ia in range(2):
            for tap in range(3):
                o = C_CW + (ia * 3 + tap) * 8
                cst[:, o:o + 8] = _pcol(cwv[ia, tap])
        cst[:, C_PS:C_PS + 8] = _pcol(np.asarray(inp["pool_scale"][0], f32))
        cst[:, C_SG:C_SG + 2] = _pcol(np.asarray(inp["attn_subln_g"][0], f32))
        cst[:, C_ML] = 1.0 if half == 1 else 0.0
        cst[:, C_MR] = 1.0 if half == 0 else 0.0
        for g, w in enumerate(POOLW):
            for i in range(8):
                for side, tl in ((0, i), (1, T - 8 + i)):
                    t = half * T + tl
                    lo = max(t - w // 2, 0)
                    hi = min(t + w - 1 - w // 2, SEQ - 1)
                    cst[:, (C_CL if side == 0 else C_CR) + g * 8 + i] = w / (hi - lo + 1)
        for h in range(4):
            cst[:, C_BC + h * 3 + 0] = rel_bias[15, h]
            cst[:, C_BC + h * 3 + 2] = rel_bias[31, h]
            cst[:, C_BC + h * 3 + 1] = rel_bias[31, h] if half == 0 else rel_bias[15, h]
        cst[:, C_EPS] = D * EPS
        cst[:, C_EPS + 1] = 256 * EPS
        m["consts"] = cst
        kl = np.arange(128)[:, None]
        y = np.arange(1152)[None, :]
        rel = kl - y + 4 * 128
        bk = _t5_bucket_np(rel)
        st = np.zeros((4, 128, 2 * 1152), f32)
        for h in range(4):
            near = rel_bias[bk, h]
            if half == 0:
                st[h, :, :1152] = near
                st[h, :, 1152:] = rel_bias[31, h]
            else:
                st[h, :, :1152] = rel_bias[15, h]
                st[h, :, 1152:] = near
        m["strips"] = st
        maps.append(m)
    return maps


_NC_CACHE = {}


def run_layers(layers, maps):
    key = tuple(layers)
    nc = build(layers)
    maps = [{k: m[k] for k in nc._used_inputs} for m in maps]
    res = run_bass_kernel_spmd(nc, maps, core_ids=list(range(8)))
    return [r["y_out"] for r in res.results]


def assemble(ys):
    out = np.zeros((4, SEQ, D), np.float32)
    for c in range(8):
        b, half = c // 2, c % 2
        yc = ys[c].reshape(128, NB, T).transpose(1, 0, 2).reshape(D, T).T
        out[b, half * T:(half + 1) * T, :] = yc
    return out


def kernel(**inputs):
    maps = prep_inputs(inputs)
    ys = run_layers(list(range(DEPTH)), maps)
    return assemble(ys)
```

```python
import math
from contextlib import ExitStack
import numpy as np
import concourse.bass as bass
import concourse.mybir as mybir
from concourse.bass_utils import run_bass_kernel_spmd

F32 = mybir.dt.float32
BF16 = mybir.dt.bfloat16
AF = mybir.ActivationFunctionType
ALU = mybir.AluOpType

D = 1024
SEQ = 4096
T = 2048
NB = 8
CH = 512
NCH = 4
DFF = 2816
FB = 22
EPS = 1e-6
DEPTH = 4
PAIRS = [[0, 1], [2, 3], [4, 5], [6, 7]]
POOLW = (2, 4, 8, 16)

C_G = 0
C_CW = 128
C_PS = 176
C_SG = 184
C_ML = 186
C_MR = 187
C_CL = 188
C_CR = 220
C_BC = 252
C_EPS = 264
NCONST = 266


def lambda_init(i):
    return 0.8 - 0.6 * math.exp(-0.3 * i)


class Ev:
    __slots__ = ("sem", "val")

    def __init__(self, sem, val):
        self.sem = sem
        self.val = val


class Res:
    def __init__(self, name):
        self.name = name
        self.w = None
        self.rs = {}
        self.dsem = None
        self.dcnt = 0


ENGS = ("pe", "act", "dve", "pool", "sp")


class Sched:
    def __init__(self, nc, es):
        self.nc = nc
        self.es = es
        self.q = {e: [] for e in ENGS}
        self.cnt = {e: 0 for e in ENGS}
        self.esem = {e: es.enter_context(nc.semaphore("es_" + e)) for e in ENGS}
        self.seen = {e: {} for e in ENGS}
        self.nsem = 0

    def res(self, name):
        return Res(name)

    def op(self, eng, fn, reads=(), writes=(), dma=False, inc=True):
        waits = {}

        def need(ev):
            if ev is not None:
                k = id(ev.sem)
                if k not in waits or waits[k].val < ev.val:
                    waits[k] = ev

        own = self.esem[eng]
        for r in reads:
            need(r.w)
        for w in writes:
            if not (eng == "pe" and w.w is not None and w.w.sem is own):
                need(w.w)
            for e in w.rs.values():
                need(e)
        seen = self.seen[eng]
        ws = []
        for k, ev in waits.items():
            if seen.get(k, 0) < ev.val:
                seen[k] = ev.val
                ws.append((ev.sem, ev.val))
        if dma:
            dst = writes[0]
            if dst.dsem is None:
                dst.dsem = self.es.enter_context(self.nc.semaphore("d%d_%s" % (self.nsem, dst.name)))
                self.nsem += 1
            dst.dcnt += 16
            ev = Ev(dst.dsem, dst.dcnt)
            amt = 16
        else:
            self.cnt[eng] += 1
            ev = Ev(own, self.cnt[eng])
            amt = 1
        for r in reads:
            r.rs[id(ev.sem)] = ev
        for w in writes:
            w.w = ev
            w.rs = {}
        self.q[eng].append((ws, fn, ev.sem, amt))
        return ev

    def final_wait(self, eng, evs):
        ws = [(e.sem, e.val) for e in evs]
        self.q[eng].append((ws, None, None, 0))

    def emit(self):
        nc = self.nc
        q = self.q

        def run(e, lst):
            for ws, fn, sem, amt in lst:
                for s, v in ws:
                    e.wait_ge(s, v)
                if fn is not None:
                    fn(e).then_inc(sem, amt)

        with nc.Block() as block:
            @block.tensor
            def _(e):
                run(e, q["pe"])

            @block.scalar
            def _(e):
                run(e, q["act"])

            @block.vector
            def _(e):
                run(e, q["dve"])

            @block.gpsimd
            def _(e):
                run(e, q["pool"])

            @block.sync
            def _(e):
                run(e, q["sp"])


STAGE = 99
SUB = 99


def build(layers, has_attn=True):
    nc = bass.Bass("TRN2", target_bir_lowering=False)
    es = ExitStack()
    S = Sched(nc, es)
    dt = nc.dram_tensor

    class Lazy:
        def __init__(self, name, shape):
            self.name, self.shape, self.h = name, shape, None

        def ap(self):
            if self.h is None:
                self.h = dt(self.name, self.shape, F32, kind="ExternalInput").ap()
                used_inputs.append(self.name)
            return self.h

        def __getitem__(self, idx):
            return self.ap()[idx]

    used_inputs = []
    nc._used_inputs = used_inputs
    x_in = Lazy("x_in", [128, NB * T]).ap()
    y_out = dt("y_out", [128, NB * T], F32, kind="ExternalOutput").ap()
    consts_d = Lazy("consts", [128, NCONST]).ap()
    wg_d = [Lazy("wg%d" % l, [FB, 128, NB * 128]) for l in range(DEPTH)]
    wu_d = [Lazy("wu%d" % l, [FB, 128, NB * 128]) for l in range(DEPTH)]
    wd_d = [Lazy("wd%d" % l, [NB, 128, FB * 128]) for l in range(DEPTH)]
    cwin_d = Lazy("cwin", [2, 24, 128, NB * 128])
    cwout_d = Lazy("cwout", [2, NB, 128, NB * 128])
    pw_d = Lazy("pw", [128, 4 * 2 * 256])
    aqk_d = Lazy("aqk", [16, 128, NB * 128])
    av_d = Lazy("av", [128, NB * 1024])
    ao_d = Lazy("ao", [NB, 128, NB * 128])
    lamv_d = Lazy("lamv", [128, 512])
    strips_d = Lazy("strips", [4, 128, 2 * 1152])

    edge_in = dt("edge_in", [128, 16], F32)
    edge_out = dt("edge_out", [2 * 128, 16], F32)
    pedge_in = dt("pedge_in", [128, 128], F32)
    pedge_out = dt("pedge_out", [2 * 128, 128], F32)
    kt_in = [dt("kt_in%d" % i, [4 * 128, T], BF16) for i in range(2)]
    kt_all = [dt("kt_all%d" % i, [2 * 4 * 128, T], BF16) for i in range(2)]
    v_in = [dt("v_in%d" % i, [8 * 128, D], BF16) for i in range(2)]
    v_all = [dt("v_all%d" % i, [2 * 8 * 128, D], BF16) for i in range(2)]

    sb = lambda name, shape, dtype: es.enter_context(nc.sbuf_tensor(name, shape, dtype))
    X = sb("X", [128, NB * T], F32)
    HN = sb("HN", [128, NB * T], BF16)
    BIG = sb("BIG", [128, FB * 1024], BF16)
    M = sb("M", [128, NB * 1024], F32)
    W8 = sb("W8", [128, 6 * 1024], BF16)
    W22 = sb("W22", [128, 2 * FB * 128], BF16)
    RSTD = sb("RSTD", [128, 2 * CH], F32)
    SQ = sb("SQ", [128, 2 * CH], BF16)
    TMP = sb("TMP", [128, 2 * CH], F32)
    CONST = sb("CONST", [128, NCONST], F32)
    ONES = sb("ONES", [128, 128], BF16)
    SMALL = sb("SMALL", [128, 160], F32)
    PS = [es.enter_context(nc.psum_tensor("ps%d" % i, [128, CH], F32)) for i in range(8)]

    r_X = [S.res("X%d" % c) for c in range(NCH)]
    r_HN = [S.res("HN%d" % c) for c in range(NCH)]
    r_PS = [S.res("ps%d" % i) for i in range(8)]
    r_W8 = [S.res("w8_%d" % i) for i in range(6)]
    r_W22 = [S.res("w22_%d" % i) for i in range(4)]
    HF = 11 * 128
    r_RSTD = [S.res("rstd%d" % i) for i in range(2)]
    r_SQ = [S.res("sq%d" % i) for i in range(2)]
    r_TMP = [S.res("tmp%d" % i) for i in range(2)]
    r_M = [S.res("M%d" % i) for i in range(2)]
    r_CONST = S.res("const")
    r_ONES = S.res("ones")
    r_SMALL = S.res("small")
    r_BIG = S.res("big")
    r_edge_in = S.res("edge_in")
    r_edge_out = S.res("edge_out")

    def xs(k, c, n=CH):
        o = k * T + c * CH
        return X[:, o:o + n]

    def hs(k, c):
        o = k * T + c * CH
        return HN[:, o:o + CH]

    def w8s(slot, k):
        o = slot * 1024 + k * 128
        return W8[:, o:o + 128]

    def w22s(slots, f):
        sl = slots[0] if f < 11 else slots[1]
        o = sl * HF + (f % 11) * 128
        return W22[:, o:o + 128]

    def cc(col, n=1):
        return CONST[:, col:col + n]

    rot = {"sq": 0, "tmp": 0, "rstd": 0, "w8": 0, "w22": 0}

    def nxt(name, n):
        v = rot[name]
        rot[name] = (v + 1) % n
        return v

    S.op("sp", lambda e: e.dma_start(out=CONST[:], in_=consts_d[:, :]), writes=[r_CONST], dma=True)
    for c in range(NCH):
        for k in range(NB):
            pass
    for c in range(NCH):
        S.op("sp", lambda e, c=c: e.dma_start(
            out=X[:].rearrange("p (k t) -> p k t", k=NB)[:, :, c * CH:(c + 1) * CH],
            in_=x_in.rearrange("p (k t) -> p k t", k=NB)[:, :, c * CH:(c + 1) * CH]),
            writes=[r_X[c]], dma=True)
    S.op("dve", lambda e: e.memset(ONES[:], 1.0), writes=[r_ONES])
    S.op("dve", lambda e: e.tensor_scalar(out=cc(C_G, 128), in0=cc(C_G, 128), scalar1=32.0, scalar2=None,
                                         op0=ALU.mult), reads=[r_CONST], writes=[r_CONST])

    def gcol(l, j, k):
        return cc(C_G + (l * 4 + j) * 8 + k)

    def rstd_from_ps(ps_i, r, dim_eps, out_ap=None, out_res=None):
        o = RSTD[:, r * CH:(r + 1) * CH] if out_ap is None else out_ap
        ores = r_RSTD[r] if out_res is None else out_res
        S.op("act", lambda e: e.activation(out=o, in_=PS[ps_i][:], func=AF.Sqrt,
                                           bias=cc(C_EPS + (0 if dim_eps > 5e-4 else 1)), scale=1.0),
             reads=[r_PS[ps_i], r_CONST], writes=[ores])
        S.op("dve", lambda e: e.reciprocal(out=o, in_=o), reads=[ores], writes=[ores])

    def sumsq_rstd(src_fn, src_res, nblk, ps_i, dimscale_eps, out_ap=None, out_res=None):
        for k in range(nblk):
            s = nxt("sq", 2)
            S.op("act", lambda e, k=k, s=s: e.activation(out=SQ[:, s * CH:(s + 1) * CH], in_=src_fn(k), func=AF.Square),
                 reads=[src_res(k)], writes=[r_SQ[s]])
            if SUB >= 2:
                S.op("pe", lambda e, k=k, s=s: e.matmul(PS[ps_i][:], ONES[:], SQ[:, s * CH:(s + 1) * CH],
                                                       start=(k == 0), stop=(k == nblk - 1)),
                     reads=[r_SQ[s], r_ONES], writes=[r_PS[ps_i]])
        r = nxt("rstd", 2)
        if SUB >= 3:
            rstd_from_ps(ps_i, r, dimscale_eps, out_ap, out_res)
        return r

    def prenorm(l, j, chunks=range(NCH)):
        for c in chunks:
            r = sumsq_rstd(lambda k, c=c: xs(k, c), lambda k, c=c: r_X[c], NB, 6 + (c % 2), D * EPS)
            for k in range(NB if SUB >= 5 else 0):
                t = nxt("tmp", 2)
                S.op("dve", lambda e, k=k, c=c, r=r, t=t: e.tensor_tensor(
                    out=TMP[:, t * CH:(t + 1) * CH], in0=xs(k, c), in1=RSTD[:, r * CH:(r + 1) * CH], op=ALU.mult),
                    reads=[r_X[c], r_RSTD[r]], writes=[r_TMP[t]])
                S.op("act", lambda e, k=k, c=c, t=t: e.activation(
                    out=hs(k, c), in_=TMP[:, t * CH:(t + 1) * CH], func=AF.Copy, scale=gcol(l, j, k)),
                    reads=[r_TMP[t], r_CONST], writes=[r_HN[c]])

    def load_w8(slot, src_ap):
        S.op("pool", lambda e: e.dma_start(out=W8[:, slot * 1024:(slot + 1) * 1024], in_=src_ap),
             writes=[r_W8[slot]], dma=True)

    def load_w22(slot, src_ap):
        S.op("pool", lambda e: e.dma_start(out=W22[:, slot * HF:(slot + 1) * HF], in_=src_ap),
             writes=[r_W22[slot]], dma=True)

    def mm_group(ps_i, terms, reads):
        n = len(terms)

        def fn(e):
            ins = None
            for i, (a, b) in enumerate(terms):
                ins = e.matmul(PS[ps_i][:], a, b, start=(i == 0), stop=(i == n - 1))
            return ins
        S.op("pe", fn, reads=reads, writes=[r_PS[ps_i]])

    def ms(d, ci):
        o = d * 1024 + ci * CH
        return M[:, o:o + CH]

    def proj_post(l, j, chunk_pairs, wload, nk, lhs_fn, rhs_fn, rhs_res, slot_res, evac_scale=None, sq_scale=None, rhs_fn_d=None):
        if evac_scale is None:
            evac_scale = lambda d: gcol(l, j, d)
        pending = []
        PBANKS = [4, 5, 0, 1, 2, 3]
        pbank = [0]
        for pair in chunk_pairs:
            for d in range(NB):
                slot = wload(d)
                for ci, c in enumerate(pair):
                    pi = PBANKS[pbank[0] % 6]
                    pbank[0] += 1
                    mm_group(pi, [(lhs_fn(slot, d, k), rhs_fn(k, c) if rhs_fn_d is None else rhs_fn_d(d, k, c)) for k in range(nk)],
                             reads=slot_res(slot) + rhs_res(c))
                    S.op("act", lambda e, d=d, ci=ci, pi=pi: e.activation(out=ms(d, ci), in_=PS[pi][:], func=AF.Copy,
                                                                       scale=evac_scale(d)),
                         reads=[r_PS[pi], r_CONST, r_SMALL], writes=[r_M[ci]])
                    s = nxt("sq", 2)
                    if sq_scale is None:
                        S.op("act", lambda e, pi=pi, s=s: e.activation(out=SQ[:, s * CH:(s + 1) * CH], in_=PS[pi][:],
                                                                    func=AF.Square),
                             reads=[r_PS[pi]], writes=[r_SQ[s]])
                    else:
                        S.op("act", lambda e, pi=pi, s=s, d=d: e.activation(out=SQ[:, s * CH:(s + 1) * CH], in_=PS[pi][:],
                                                                         func=AF.Square, scale=sq_scale(d)),
                             reads=[r_PS[pi], r_CONST], writes=[r_SQ[s]])
                    for fnp in pending:
                        fnp()
                    del pending[:]
                    pending.append(lambda d=d, ci=ci, s=s: S.op(
                        "pe", lambda e: e.matmul(PS[6 + ci][:], ONES[:], SQ[:, s * CH:(s + 1) * CH],
                                                 start=(d == 0), stop=(d == NB - 1)),
                        reads=[r_SQ[s], r_ONES], writes=[r_PS[6 + ci]]))
            for fnp in pending:
                fnp()
            del pending[:]
            for ci, c in enumerate(pair):
                r = nxt("rstd", 2)
                rstd_from_ps(6 + ci, r, D * EPS)
                for d in range(NB):
                    t = nxt("tmp", 2)
                    S.op("dve", lambda e, d=d, ci=ci, r=r, t=t: e.tensor_tensor(
                        out=TMP[:, t * CH:(t + 1) * CH], in0=ms(d, ci), in1=RSTD[:, r * CH:(r + 1) * CH], op=ALU.mult),
                        reads=[r_M[ci], r_RSTD[r]], writes=[r_TMP[t]])
                    S.op("dve", lambda e, d=d, c=c, t=t: e.tensor_tensor(
                        out=xs(d, c), in0=xs(d, c), in1=TMP[:, t * CH:(t + 1) * CH], op=ALU.add),
                        reads=[r_TMP[t], r_X[c]], writes=[r_X[c]])

    def w8_loader(src_fn):
        def wload(d):
            slot = nxt("w8", 6)
            load_w8(slot, src_fn(d))
            return slot
        return wload

    def a_s(f, ci):
        o = f * 1024 + ci * CH
        return BIG[:, o:o + CH]

    r_A = [[S.res("a%d_%d" % (f, ci)) for ci in range(2)] for f in range(FB)]

    def ffn(l):
        for hf in range(2):
            pair = (2 * hf, 2 * hf + 1)
            prenorm(l, 2, pair)
            for f in range(FB):
                sg_ = nxt("w8", 6)
                load_w8(sg_, wg_d[l][f])
                su_ = nxt("w8", 6)
                load_w8(su_, wu_d[l][f])
                for ci, c in enumerate(pair):
                    pg, pu = ci, 2 + ci
                    mm_group(pg, [(w8s(sg_, k), hs(k, c)) for k in range(NB)], reads=[r_W8[sg_], r_HN[c]])
                    mm_group(pu, [(w8s(su_, k), hs(k, c)) for k in range(NB)], reads=[r_W8[su_], r_HN[c]])
                    t = nxt("tmp", 2)
                    S.op("act", lambda e, pg=pg, t=t: e.activation(out=TMP[:, t * CH:(t + 1) * CH], in_=PS[pg][:], func=AF.Silu),
                         reads=[r_PS[pg]], writes=[r_TMP[t]])
                    S.op("dve", lambda e, f=f, ci=ci, pu=pu, t=t: e.tensor_tensor(
                        out=a_s(f, ci), in0=TMP[:, t * CH:(t + 1) * CH], in1=PS[pu][:], op=ALU.mult),
                        reads=[r_TMP[t], r_PS[pu]], writes=[r_A[f][ci]])

            def wload(d):
                s0 = nxt("w22", 4)
                load_w22(s0, wd_d[l][d][:, 0:HF])
                s1 = nxt("w22", 4)
                load_w22(s1, wd_d[l][d][:, HF:2 * HF])
                return (s0, s1)
            proj_post(l, 3, [pair], wload, FB,
                      lambda slot, d, k: w22s(slot, k),
                      lambda k, c, hf=hf: a_s(k, c - 2 * hf),
                      lambda c, hf=hf: [r_A[f][c - 2 * hf] for f in range(FB)],
                      lambda slots: [r_W22[slots[0]], r_W22[slots[1]]])

    r_fs = S.res("fence_scratch")

    def fence(reads, writes):
        S.op("dve", lambda e: e.memset(SMALL[:, 159:160], 0.0), reads=list(reads), writes=list(writes) + [r_fs])

    all_A = [r_A[f][ci] for f in range(FB) for ci in range(2)]

    def conv_mixer(l, ia):
        if STAGE < 2:
            return
        prenorm(l, 0)
        if STAGE < 3:
            return
        def U(s, a, n):
            o = s * T + a
            return M[:, o:o + n]

        def Bf(s, a, n):
            o = 2 * T + s * T + a
            return M[:, o:o + n]
        TT = BIG[:, 8 * T: 8 * T + 2 * T].bitcast(F32)
        r_U = [S.res("U0"), S.res("U1")]
        r_B = [S.res("B0"), S.res("B1")]
        T2 = BIG[:, 8 * T + 2 * T: 8 * T + 2 * T + 2048].bitcast(F32)
        r_TT = S.res("TT")
        r_T2 = S.res("T2")
        r_V = S.res("V")
        fence(all_A + r_M, r_U + r_B + [r_TT, r_T2, r_V])

        def cw(tap, f):
            return cc(C_CW + (ia * 3 + tap) * 8 + f)
        for f in range(NB):
            s3 = [nxt("w8", 6) for _ in range(3)]
            for i, sl in enumerate(s3):
                load_w8(sl, cwin_d[ia, i * 8 + f])
            us = f % 2
            for c in range(NCH):
                pb, pc, ph = (0, 1, 2) if c % 2 == 0 else (3, 4, 5)
                for pi, sl in zip((pb, pc, ph), s3):
                    mm_group(pi, [(w8s(sl, k), hs(k, c)) for k in range(NB)], reads=[r_W8[sl], r_HN[c]])
                t = nxt("tmp", 2)
                S.op("act", lambda e, ph=ph, t=t: e.activation(out=TMP[:, t * CH:(t + 1) * CH], in_=PS[ph][:], func=AF.Copy),
                     reads=[r_PS[ph]], writes=[r_TMP[t]])
                S.op("dve", lambda e, pc=pc, t=t, us=us, c=c: e.tensor_tensor(
                    out=U(us, c * CH, CH), in0=TMP[:, t * CH:(t + 1) * CH], in1=PS[pc][:], op=ALU.mult),
                    reads=[r_TMP[t], r_PS[pc]], writes=[r_U[us]])
                S.op("act", lambda e, pb=pb, us=us, c=c: e.activation(out=Bf(us, c * CH, CH), in_=PS[pb][:], func=AF.Copy),
                     reads=[r_PS[pb]], writes=[r_B[us]])
            S.op("dve", lambda e, us=us, f=f: e.tensor_scalar(out=TT, in0=U(us, 0, T), scalar1=cw(1, f), scalar2=None,
                                                           op0=ALU.mult),
                 reads=[r_U[us], r_CONST], writes=[r_TT])
            for tap, (oa, ob, ia_) in ((0, (1, 1024, 0)), (0, (1024, T, 1023)), (2, (0, 1024, 1)), (2, (1024, T - 1, 1025))):
                n = ob - oa
                S.op("act", lambda e, us=us, f=f, tap=tap, ia_=ia_, n=n: e.activation(
                    out=T2[:, 0:n], in_=U(us, ia_, n), func=AF.Copy, scale=cw(tap, f)),
                    reads=[r_U[us], r_CONST], writes=[r_T2])
                S.op("dve", lambda e, oa=oa, ob=ob, n=n: e.tensor_tensor(out=TT[:, oa:ob], in0=TT[:, oa:ob], in1=T2[:, 0:n],
                                                                       op=ALU.add),
                     reads=[r_T2, r_TT], writes=[r_TT])
            S.op("dve", lambda e, us=us, f=f: e.tensor_tensor(out=BIG[:, f * T:(f + 1) * T], in0=TT, in1=Bf(us, 0, T),
                                                            op=ALU.mult),
                 reads=[r_TT, r_B[us]], writes=[r_V])
            for j, tcol in enumerate((0, T - 1)):
                S.op("dve", lambda e, us=us, f=f, j=j, tcol=tcol: e.tensor_copy(
                    out=SMALL[:, 2 * f + j:2 * f + j + 1], in_=U(us, tcol, 1)), reads=[r_U[us]], writes=[r_SMALL])
                S.op("dve", lambda e, f=f, j=j, tcol=tcol: e.tensor_copy(
                    out=SMALL[:, 16 + 2 * f + j:16 + 2 * f + j + 1], in_=TT[:, tcol:tcol + 1]), reads=[r_TT], writes=[r_SMALL])
                S.op("dve", lambda e, us=us, f=f, j=j, tcol=tcol: e.tensor_copy(
                    out=SMALL[:, 32 + 2 * f + j:32 + 2 * f + j + 1], in_=Bf(us, tcol, 1)), reads=[r_B[us]], writes=[r_SMALL])
        if STAGE < 4:
            return
        S.op("sp", lambda e: e.dma_start(out=edge_in[:, 0:16], in_=SMALL[:, 0:16]), reads=[r_SMALL], writes=[r_edge_in], dma=True)
        halo_exchange()
        S.op("sp", lambda e: e.dma_start(out=SMALL[:, 48:80].rearrange("p (r n) -> p r n", r=2),
                                         in_=edge_out.ap().rearrange("(r p) n -> p r n", p=128)[:, :, 0:16]),
             reads=[r_edge_out], writes=[r_SMALL], dma=True)
        sm3 = lambda o: SMALL[:, o:o + 16].rearrange("p (f two) -> p f two", two=2)
        cw3 = lambda tap: CONST[:, C_CW + (ia * 3 + tap) * 8: C_CW + (ia * 3 + tap) * 8 + 8]
        for side, (hoff, hidx, mcol, tap, tcol) in enumerate(((48, 1, C_ML, 0, 0), (64, 0, C_MR, 2, T - 1))):
            hl = SMALL[:, 80 + 8 * side: 88 + 8 * side]
            S.op("dve", lambda e, hl=hl, hoff=hoff, hidx=hidx, mcol=mcol: e.tensor_scalar(
                out=hl, in0=sm3(hoff)[:, :, hidx], scalar1=cc(mcol), scalar2=None, op0=ALU.mult),
                reads=[r_SMALL, r_CONST], writes=[r_SMALL])
            S.op("dve", lambda e, hl=hl, tap=tap: e.tensor_tensor(out=hl, in0=hl, in1=cw3(tap), op=ALU.mult),
                 reads=[r_SMALL, r_CONST], writes=[r_SMALL])
            S.op("dve", lambda e, hl=hl, side=side: e.tensor_tensor(out=hl, in0=hl, in1=sm3(16)[:, :, side], op=ALU.add),
                 reads=[r_SMALL], writes=[r_SMALL])
            S.op("dve", lambda e, hl=hl, side=side, tcol=tcol: e.tensor_tensor(
                out=BIG[:, 0:NB * T].rearrange("p (f t) -> p f t", f=NB)[:, :, tcol],
                in0=hl, in1=sm3(32)[:, :, side], op=ALU.mult),
                reads=[r_SMALL, r_V], writes=[r_V])
        fence(r_U + r_B + [r_TT, r_T2], r_M)
        proj_post(l, 1, [(1, 2), (0, 3)], w8_loader(lambda d: cwout_d[ia, d]), NB,
                  lambda slot, d, k: w8s(slot, k),
                  lambda k, c: BIG[:, k * T + c * CH: k * T + (c + 1) * CH],
                  lambda c: [r_V],
                  lambda slot: [r_W8[slot]])
        fence([r_V, r_TT, r_T2], all_A)


    def pool_mixer(l):
        HW = T + 16
        RS_ALL = BIG[:, 0:2 * T].bitcast(F32)
        PEDGE = M[:, 0:128]
        PHALO = M[:, 128:384]
        HP = M[:, 384:384 + HW]
        SA = M[:, 384 + HW:384 + 2 * HW]
        SB = M[:, 384 + 2 * HW:384 + 3 * HW]
        r_RS, r_PE, r_PH, r_HP, r_SA, r_SB = (S.res(n) for n in ("RSALL", "PEDGE", "PHALO", "HP", "SA", "SB"))
        r_pin, r_pout, r_PW = S.res("pedge_in"), S.res("pedge_out"), S.res("PW")
        fence(all_A + r_M, [r_RS, r_PE, r_PH, r_HP, r_SA, r_SB])
        PW = W22[:, 0:2048]
        S.op("pool", lambda e: e.dma_start(out=PW, in_=pw_d.ap()), writes=[r_PW] + r_W22, dma=True)
        S.op("dve", lambda e: e.tensor_tensor(out=SMALL[:, 100:108], in0=cc(C_PS, 8), in1=cc(C_G + (l * 4 + 1) * 8, 8), op=ALU.mult),
             reads=[r_CONST, r_SMALL], writes=[r_SMALL])
        for c in range(NCH):
            sumsq_rstd(lambda k, c=c: xs(k, c), lambda k, c=c: r_X[c], NB, 6 + (c % 2), D * EPS,
                       out_ap=RS_ALL[:, c * CH:(c + 1) * CH], out_res=r_RS)
        for f in range(NB):
            for side, a in enumerate((0, T - 8)):
                o = f * 16 + side * 8
                S.op("dve", lambda e, f=f, a=a, o=o: e.tensor_tensor(out=PEDGE[:, o:o + 8], in0=X[:, f * T + a:f * T + a + 8],
                                                                  in1=RS_ALL[:, a:a + 8], op=ALU.mult),
                     reads=[r_X[0], r_X[3], r_RS], writes=[r_PE])
                S.op("dve", lambda e, f=f, o=o: e.tensor_scalar(out=PEDGE[:, o:o + 8], in0=PEDGE[:, o:o + 8],
                                                             scalar1=gcol(l, 0, f), scalar2=None, op0=ALU.mult),
                     reads=[r_PE, r_CONST], writes=[r_PE])
        S.op("sp", lambda e: e.dma_start(out=pedge_in[:, :], in_=PEDGE), reads=[r_PE], writes=[r_pin], dma=True)
        S.op("pool", lambda e: e.collective_compute("AllGather", ALU.bypass, replica_groups=PAIRS,
                                                    ins=[pedge_in.ap().opt()], outs=[pedge_out.ap().opt()]),
             reads=[r_pin], writes=[r_pout])
        S.op("sp", lambda e: e.dma_start(out=PHALO.rearrange("p (r n) -> p r n", r=2),
                                         in_=pedge_out.ap().rearrange("(r p) n -> p r n", p=128)),
             reads=[r_pout], writes=[r_PH], dma=True)
        for f in range(NB):
            g = f // 2
            w = POOLW[g]
            S.op("dve", lambda e, f=f: e.tensor_scalar(out=HP[:, 0:8], in0=PHALO[:, f * 16 + 8:f * 16 + 16], scalar1=cc(C_ML),
                                                     scalar2=None, op0=ALU.mult), reads=[r_PH, r_CONST], writes=[r_HP])
            S.op("dve", lambda e, f=f: e.tensor_scalar(out=HP[:, 8 + T:16 + T], in0=PHALO[:, 128 + f * 16:128 + f * 16 + 8],
                                                     scalar1=cc(C_MR), scalar2=None, op0=ALU.mult),
                 reads=[r_PH, r_CONST], writes=[r_HP])
            S.op("dve", lambda e, f=f: e.tensor_tensor(out=HP[:, 8:8 + T], in0=X[:, f * T:(f + 1) * T], in1=RS_ALL, op=ALU.mult),
                 reads=r_X + [r_RS], writes=[r_HP])
            S.op("dve", lambda e, f=f: e.tensor_scalar(out=HP[:, 8:8 + T], in0=HP[:, 8:8 + T], scalar1=gcol(l, 0, f),
                                                     scalar2=None, op0=ALU.mult), reads=[r_HP, r_CONST], writes=[r_HP])
            S.op("dve", lambda e: e.tensor_tensor(out=SA[:, 1:HW], in0=HP[:, 0:HW - 1], in1=HP[:, 1:HW], op=ALU.add),
                 reads=[r_HP], writes=[r_SA])
            cur, cur_r, oth, oth_r = SA, r_SA, SB, r_SB
            if w >= 4:
                S.op("dve", lambda e: e.tensor_tensor(out=SB[:, 2:HW - 1], in0=SA[:, 1:HW - 2], in1=SA[:, 3:HW], op=ALU.add),
                     reads=[r_SA], writes=[r_SB])
                cur, cur_r, oth, oth_r = SB, r_SB, SA, r_SA
            if w >= 8:
                S.op("dve", lambda e: e.tensor_tensor(out=SA[:, 4:HW - 3], in0=SB[:, 2:HW - 5], in1=SB[:, 6:HW - 1], op=ALU.add),
                     reads=[r_SB], writes=[r_SA])
                cur, cur_r, oth, oth_r = SA, r_SA, SB, r_SB
            if w >= 16:
                S.op("dve", lambda e: e.tensor_tensor(out=SB[:, 8:HW - 7], in0=SA[:, 4:HW - 11], in1=SA[:, 12:HW - 3], op=ALU.add),
                     reads=[r_SA], writes=[r_SB])
                cur, cur_r, oth, oth_r = SB, r_SB, SA, r_SA
            S.op("dve", lambda e, cur=cur, g=g: e.tensor_tensor(out=cur[:, 8:16], in0=cur[:, 8:16], in1=cc(C_CL + g * 8, 8), op=ALU.mult),
                 reads=[cur_r, r_CONST], writes=[cur_r])
            S.op("dve", lambda e, cur=cur, g=g: e.tensor_tensor(out=cur[:, T:T + 8], in0=cur[:, T:T + 8], in1=cc(C_CR + g * 8, 8), op=ALU.mult),
                 reads=[cur_r, r_CONST], writes=[cur_r])
            S.op("act", lambda e, cur=cur, oth=oth, w=w: e.activation(out=oth[:, 8:8 + T], in_=cur[:, 8:8 + T], func=AF.Copy,
                                                                    scale=1.0 / w), reads=[cur_r], writes=[oth_r])
            S.op("dve", lambda e, oth=oth, f=f: e.tensor_tensor(out=HN[:, f * T:(f + 1) * T], in0=oth[:, 8:8 + T], in1=HP[:, 8:8 + T],
                                                              op=ALU.subtract), reads=[oth_r, r_HP], writes=r_HN)
        fence([r_PE, r_PH, r_HP, r_SA, r_SB], r_M)

        def lhs(slot, d, k):
            g, dblk = d // 2, d % 2
            o = (g * 2 + k) * 256 + dblk * 128
            return PW[:, o:o + 128]
        proj_post(l, 1, [(0, 1), (2, 3)], lambda d: 0, 2, lhs,
                  lambda k, c: None, lambda c: [r_HN[c]], lambda slot: [r_PW],
                  evac_scale=lambda d: SMALL[:, 100 + d:101 + d], sq_scale=lambda d: cc(C_PS + d),
                  rhs_fn_d=lambda d, k, c: hs((d // 2) * 2 + k, c))
        fence([r_RS, r_PW], all_A + r_W22)

    def attn_mixer(l):
        li = lambda_init(l)
        MB = M[:].bitcast(BF16)
        WF = W22[:].bitcast(F32)
        r_lam = S.res("lamtmp")
        fence(r_TMP, [r_lam])
        S.op("sp", lambda e: e.dma_start(out=TMP[:, 0:512], in_=lamv_d.ap()), writes=[r_lam, r_TMP[0]], dma=True)
        for i in range(2):
            S.op("dve", lambda e, i=i: e.tensor_tensor(out=TMP[:, 512 + 128 * i:640 + 128 * i], in0=TMP[:, 256 * i:256 * i + 128],
                                                     in1=TMP[:, 256 * i + 128:256 * i + 256], op=ALU.mult),
                 reads=[r_lam], writes=[r_TMP[1]])
            S.op("dve", lambda e, i=i: e.reduce_sum(out=SMALL[:, 110 + i:111 + i], in_=TMP[:, 512 + 128 * i:640 + 128 * i],
                                                  axis=mybir.AxisListType.X), reads=[r_TMP[1]], writes=[r_SMALL])
        S.op("act", lambda e: e.activation(out=SMALL[:, 112:114], in_=SMALL[:, 110:112], func=AF.Exp), reads=[r_SMALL], writes=[r_SMALL])
        S.op("dve", lambda e: e.tensor_tensor(out=SMALL[:, 114:115], in0=SMALL[:, 112:113], in1=SMALL[:, 113:114], op=ALU.subtract),
             reads=[r_SMALL], writes=[r_SMALL])
        S.op("dve", lambda e: e.tensor_scalar(out=SMALL[:, 115:116], in0=SMALL[:, 114:115], scalar1=li, scalar2=-1.0,
                                            op0=ALU.add, op1=ALU.mult), reads=[r_SMALL], writes=[r_SMALL])
        S.op("dve", lambda e: e.tensor_scalar(out=SMALL[:, 116:118], in0=cc(C_SG, 2), scalar1=16.0 * (1.0 - li), scalar2=None,
                                            op0=ALU.mult), reads=[r_SMALL, r_CONST], writes=[r_SMALL])
        fence([r_lam], r_TMP)
        neglam = SMALL[:, 115:116]
        prenorm(l, 0)
        r_Q = S.res("QT")
        r_KST = [S.res("kst0"), S.res("kst1")]
        r_VST = [S.res("vst0"), S.res("vst1")]
        r_WV = S.res("WV")
        r_ktin = [S.res("kt_in0"), S.res("kt_in1")]
        r_ktall = [S.res("kt_all0"), S.res("kt_all1")]
        r_vin = [S.res("v_in0"), S.res("v_in1")]
        r_vall = [S.res("v_all0"), S.res("v_all1")]

        def gather(src, dst, r_src, r_dst):
            S.op("pool", lambda e: e.collective_compute("AllGather", ALU.bypass, replica_groups=PAIRS,
                                                        ins=[src.ap().opt()], outs=[dst.ap().opt()]),
                 reads=[r_src], writes=[r_dst])
        fence(all_A + r_M, [r_Q, r_WV] + r_KST + r_VST)
        WV = MB[:, 4096:12288]
        for i in range(2):
            S.op("pool", lambda e, i=i: e.dma_start(out=WV[:, i * 4096:(i + 1) * 4096], in_=av_d.ap()[:, i * 4096:(i + 1) * 4096]),
                 writes=[r_WV], dma=True)
        qscale = 128 ** -0.5
        bank = [0]

        def nb():
            bank[0] = (bank[0] + 1) % 4
            return bank[0]
        for j in range(NB):
            sl = nxt("w8", 6)
            load_w8(sl, aqk_d[j])
            for c in range(NCH):
                pi = nb()
                mm_group(pi, [(w8s(sl, k), hs(k, c)) for k in range(NB)], reads=[r_W8[sl], r_HN[c]])
                S.op("act", lambda e, pi=pi, j=j, c=c: e.activation(out=BIG[:, j * T + c * CH:j * T + (c + 1) * CH], in_=PS[pi][:],
                                                                  func=AF.Copy, scale=qscale), reads=[r_PS[pi]], writes=[r_Q])
        for j in range(NB):
            sl = nxt("w8", 6)
            load_w8(sl, aqk_d[8 + j])
            ks = j % 2
            for c in range(NCH):
                pi = nb()
                mm_group(pi, [(w8s(sl, k), hs(k, c)) for k in range(NB)], reads=[r_W8[sl], r_HN[c]])
                S.op("act", lambda e, pi=pi, ks=ks, c=c: e.activation(out=MB[:, ks * T + c * CH:ks * T + (c + 1) * CH], in_=PS[pi][:],
                                                                   func=AF.Copy), reads=[r_PS[pi]], writes=[r_KST[ks]])
            S.op("sp", lambda e, j=j, ks=ks: e.dma_start(out=kt_in[j // 4][(j % 4) * 128:(j % 4 + 1) * 128, :],
                                                       in_=MB[:, ks * T:(ks + 1) * T]),
                 reads=[r_KST[ks]], writes=[r_ktin[j // 4]], dma=True)
            if j % 4 == 3:
                gather(kt_in[j // 4], kt_all[j // 4], r_ktin[j // 4], r_ktall[j // 4])
        for tb in range(16):
            vs = tb % 2
            for ec in range(2):
                pi = nb()
                mm_group(pi, [(HN[:, k * T + tb * 128:k * T + (tb + 1) * 128], WV[:, k * 1024 + ec * 512:k * 1024 + (ec + 1) * 512])
                              for k in range(NB)], reads=[r_WV] + r_HN)
                S.op("act", lambda e, pi=pi, vs=vs, ec=ec: e.activation(
                    out=MB[:, 12288 + vs * 1024 + ec * 512:12288 + vs * 1024 + (ec + 1) * 512], in_=PS[pi][:], func=AF.Copy),
                    reads=[r_PS[pi]], writes=[r_VST[vs]])
            S.op("sp", lambda e, tb=tb, vs=vs: e.dma_start(out=v_in[tb // 8][(tb % 8) * 128:(tb % 8 + 1) * 128, :],
                                                         in_=MB[:, 12288 + vs * 1024:12288 + (vs + 1) * 1024]),
                 reads=[r_VST[vs]], writes=[r_vin[tb // 8]], dma=True)
            if tb % 8 == 7:
                gather(v_in[tb // 8], v_all[tb // 8], r_vin[tb // 8], r_vall[tb // 8])
        r_KT = [S.res("KT0"), S.res("KT1")]
        r_V = S.res("Vh")
        r_ST = S.res("strips")
        r_ON0, r_OD = S.res("On0"), S.res("Od")
        r_P = [S.res("P%d" % i) for i in range(6)]
        fence(r_KST + r_VST + [r_WV] + r_W22, r_KT + [r_V, r_ON0, r_OD] + r_P)
        fence(all_A, [r_ST] + r_P)
        STRIP = BIG[:, 8 * T:8 * T + 2 * 2304].bitcast(F32)
        KT = lambda t: MB[:, t * 4096:(t + 1) * 4096]
        VH = MB[:, 8192:16384]
        ON0 = lambda eb: WF[:, eb * CH:(eb + 1) * CH]
        OD = lambda eb: WF[:, 1024 + eb * CH:1024 + (eb + 1) * CH]
        PT = lambda i: (WF[:, 2048 + 256 * i:2048 + 256 * (i + 1)].bitcast(BF16) if i < 3
                        else BIG[:, 20992 + (i - 3) * CH:20992 + (i - 2) * CH])
        SB_ = [0, 1, 6, 7]
        LOOK = 3
        items = [(h, qc, t) for h in range(4) for qc in range(NCH) for t in range(2)]
        steps = [(it, kb) for it in range(len(items)) for kb in range(32)]
        NS = len(steps)
        deferred = []

        def head_loads(h):
            S.op("sp", lambda e, h=h: e.dma_start(out=STRIP, in_=strips_d[h]), writes=[r_ST], dma=True)
            for r in range(2):
                for hv in range(2):
                    b0 = r * 16 + hv * 8
                    S.op("sp", lambda e, h=h, r=r, hv=hv, b0=b0: e.dma_start(
                        out=VH[:, b0 * 256:(b0 + 8) * 256].rearrange("p (b e) -> p b e", e=256),
                        in_=v_all[hv].ap().rearrange("(b p) e -> p b e", p=128)[:, r * 8:(r + 1) * 8, h * 256:(h + 1) * 256]),
                        reads=[r_vall[hv]], writes=[r_V], dma=True)
            for t in range(2):
                j = 2 * h + t
                for r in range(2):
                    S.op("sp", lambda e, t=t, j=j, r=r: e.dma_start(
                        out=KT(t)[:, r * T:(r + 1) * T],
                        in_=kt_all[j // 4][r * 512 + (j % 4) * 128:r * 512 + (j % 4 + 1) * 128, :]),
                        reads=[r_ktall[j // 4]], writes=[r_KT[t]], dma=True)

        def emit_S(si):
            it, kb = steps[si]
            h, qc, t = items[it]
            if kb == 0 and t == 0 and qc == 0:
                head_loads(h)
            j = 2 * h + t
            sbk = SB_[si % 4]
            mm_group(sbk, [(KT(t)[:, kb * 128:(kb + 1) * 128], BIG[:, j * T + qc * CH:j * T + (qc + 1) * CH])],
                     reads=[r_KT[t], r_Q])
            pi = si % 6
            da, dbb = kb - 4 * qc + 1, kb - 16 - 4 * qc + 1
            if 0 <= da <= 5 or 0 <= dbb <= 5:
                off = (5 - da) * 128 if 0 <= da <= 5 else 1152 + (5 - dbb) * 128
                tt = nxt("tmp", 2)
                S.op("dve", lambda e, sbk=sbk, off=off, tt=tt: e.tensor_tensor(
                    out=TMP[:, tt * CH:(tt + 1) * CH], in0=PS[sbk][:], in1=STRIP[:, off:off + CH], op=ALU.add),
                    reads=[r_PS[sbk], r_ST], writes=[r_TMP[tt]])
                S.op("act", lambda e, tt=tt, pi=pi: e.activation(out=PT(pi), in_=TMP[:, tt * CH:(tt + 1) * CH], func=AF.Exp),
                     reads=[r_TMP[tt]], writes=[r_P[pi]])
            else:
                cls = 0 if da < 0 else (2 if dbb > 5 else 1)
                S.op("act", lambda e, sbk=sbk, pi=pi, h=h, cls=cls: e.activation(
                    out=PT(pi), in_=PS[sbk][:], func=AF.Exp, bias=cc(C_BC + h * 3 + cls), scale=1.0),
                    reads=[r_PS[sbk], r_CONST], writes=[r_P[pi]])

        def emit_PV(si):
            it, kb = steps[si]
            h, qc, t = items[it]
            pi = si % 6
            for eb in range(2):
                S.op("pe", lambda e, eb=eb, kb=kb, pi=pi: e.matmul(
                    PS[2 + eb][:], VH[:, kb * 256 + eb * 128:kb * 256 + (eb + 1) * 128], PT(pi),
                    start=(kb == 0), stop=(kb == 31)), reads=[r_V, r_P[pi]], writes=[r_PS[2 + eb]])
            S.op("pe", lambda e, kb=kb, pi=pi: e.matmul(PS[4][:], ONES[:], PT(pi), start=(kb == 0), stop=(kb == 31)),
                 reads=[r_ONES, r_P[pi]], writes=[r_PS[4]])
            if kb == 31:
                item_end(si, h, qc, t)

        def item_end(si, h, qc, t):
            rr = nxt("rstd", 2)
            S.op("dve", lambda e, rr=rr: e.reciprocal(out=RSTD[:, rr * CH:(rr + 1) * CH], in_=PS[4][:]),
                 reads=[r_PS[4]], writes=[r_RSTD[rr]])
            dst, r_dst = (ON0, r_ON0) if t == 0 else (OD, r_OD)
            for eb in range(2):
                S.op("act", lambda e, eb=eb, dst=dst: e.activation(out=dst(eb), in_=PS[2 + eb][:], func=AF.Copy),
                     reads=[r_PS[2 + eb]], writes=[r_dst])
            for eb in range(2):
                S.op("dve", lambda e, eb=eb, rr=rr, dst=dst: e.tensor_tensor(out=dst(eb), in0=dst(eb),
                                                                           in1=RSTD[:, rr * CH:(rr + 1) * CH], op=ALU.mult),
                     reads=[r_dst, r_RSTD[rr]], writes=[r_dst])
                if t == 1:
                    S.op("dve", lambda e, eb=eb: e.tensor_scalar(out=OD(eb), in0=OD(eb), scalar1=neglam, scalar2=None, op0=ALU.mult),
                         reads=[r_OD, r_SMALL], writes=[r_OD])
                    S.op("dve", lambda e, eb=eb: e.tensor_tensor(out=OD(eb), in0=OD(eb), in1=ON0(eb), op=ALU.add),
                         reads=[r_OD, r_ON0], writes=[r_OD])
            if t == 1:
                sqs = []
                for eb in range(2):
                    sl = nxt("sq", 2)
                    sqs.append(sl)
                    S.op("act", lambda e, eb=eb, sl=sl: e.activation(out=SQ[:, sl * CH:(sl + 1) * CH], in_=OD(eb), func=AF.Square),
                         reads=[r_OD], writes=[r_SQ[sl]])

                def later(h=h, qc=qc, sqs=sqs):
                    for eb in range(2):
                        sl = sqs[eb]
                        S.op("pe", lambda e, eb=eb, sl=sl: e.matmul(PS[5][:], ONES[:], SQ[:, sl * CH:(sl + 1) * CH],
                                                                  start=(eb == 0), stop=(eb == 1)),
                             reads=[r_SQ[sl], r_ONES], writes=[r_PS[5]])
                    rr2 = nxt("rstd", 2)
                    rstd_from_ps(5, rr2, 256 * EPS)
                    for eb in range(2):
                        tt = nxt("tmp", 2)
                        S.op("dve", lambda e, eb=eb, rr2=rr2, tt=tt: e.tensor_tensor(
                            out=TMP[:, tt * CH:(tt + 1) * CH], in0=OD(eb), in1=RSTD[:, rr2 * CH:(rr2 + 1) * CH], op=ALU.mult),
                            reads=[r_OD, r_RSTD[rr2]], writes=[r_TMP[tt]])
                        S.op("act", lambda e, eb=eb, tt=tt, h=h, qc=qc: e.activation(
                            out=hs(2 * h + eb, qc), in_=TMP[:, tt * CH:(tt + 1) * CH], func=AF.Copy,
                            scale=SMALL[:, 116 + eb:117 + eb]), reads=[r_TMP[tt], r_SMALL], writes=[r_HN[qc]])
                deferred.append((si + 10, later))

        npv = 0
        for si in range(NS + LOOK):
            if si < NS:
                if si > 0 and si % (NCH * 2 * 32) == 0:
                    while npv < si:
                        emit_PV(npv)
                        npv += 1
                emit_S(si)
            if si - LOOK >= 0 and npv <= si - LOOK:
                emit_PV(npv)
                npv += 1
            while deferred and deferred[0][0] <= npv - 1:
                deferred.pop(0)[1]()
        while npv < NS:
            emit_PV(npv)
            npv += 1
        while deferred:
            deferred.pop(0)[1]()
        fence(r_KT + [r_V, r_ON0, r_OD] + r_P, r_M + r_W22)
        proj_post(l, 1, [(0, 1), (2, 3)], w8_loader(lambda d: ao_d[d]), NB,
                  lambda slot, d, k: w8s(slot, k), lambda k, c: hs(k, c), lambda c: [r_HN[c]], lambda slot: [r_W8[slot]])
        fence([r_Q, r_ST] + r_P, all_A)

    r_cc = S.res("cc")

    def halo_exchange():
        S.op("pool", lambda e: e.collective_compute("AllGather", ALU.bypass, replica_groups=PAIRS,
                                                    ins=[edge_in.ap().opt()], outs=[edge_out.ap().opt()]),
             reads=[r_edge_in], writes=[r_edge_out])

    for l in layers:
        kind = l % 3
        if kind == 0:
            conv_mixer(l, l // 3)
        elif kind == 1:
            pool_mixer(l)
        else:
            attn_mixer(l)
        if STAGE >= 5:
            ffn(l)

    r_out = [S.res("out%d" % c) for c in range(NCH)]
    evs = []
    for c in range(NCH):
        evs.append(S.op("sp", lambda e, c=c: e.dma_start(
            out=y_out.rearrange("p (k t) -> p k t", k=NB)[:, :, c * CH:(c + 1) * CH],
            in_=X[:].rearrange("p (k t) -> p k t", k=NB)[:, :, c * CH:(c + 1) * CH]),
            reads=[r_X[c]], writes=[r_out[c]], dma=True))
    S.final_wait("sp", evs)
    S.emit()
    es.close()
    return nc


def _tile_kn(w, kb):
    K, N = w.shape
    return np.ascontiguousarray(w.reshape(kb, 128, N // 128, 128).transpose(2, 1, 0, 3).reshape(N // 128, 128, kb * 128))


def _pcol(v):
    return v.reshape(-1, 128).T


def _t5_bucket_np(rel):
    half = 16
    ret = np.where(rel > 0, half, 0)
    n = np.abs(rel)
    max_exact = 8
    nf = np.maximum(n, 1).astype(np.float32)
    large = max_exact + (np.log(nf / max_exact) / math.log(128 / max_exact) * (half - max_exact)).astype(np.int32)
    large = np.minimum(large, half - 1)
    return ret + np.where(n < max_exact, n, large)


def prep_inputs(inp):
    f32 = np.float32
    x = np.asarray(inp["x"], f32)
    shared = {}
    for l in range(DEPTH):
        shared["wg%d" % l] = _tile_kn(np.asarray(inp["ffn_w_gate"][l], f32), 8)
        shared["wu%d" % l] = _tile_kn(np.asarray(inp["ffn_w_up"][l], f32), 8)
        shared["wd%d" % l] = _tile_kn(np.asarray(inp["ffn_w_down"][l], f32), FB)
    shared["cwin"] = np.stack([_tile_kn(np.asarray(inp["conv_w_in"][i], f32), 8) for i in range(2)])
    shared["cwout"] = np.stack([_tile_kn(np.asarray(inp["conv_w_out"][i], f32), 8) for i in range(2)])
    pw = np.asarray(inp["pool_w"][0], f32)
    shared["pw"] = np.ascontiguousarray(pw.reshape(4, 2, 128, 256).transpose(2, 0, 1, 3).reshape(128, 4 * 2 * 256))
    wqkv = np.asarray(inp["attn_w_qkv"][0], f32)
    shared["aqk"] = _tile_kn(wqkv[:, :2048], 8)
    shared["av"] = np.ascontiguousarray(wqkv[:, 2048:].reshape(8, 128, 1024).transpose(1, 0, 2).reshape(128, 8 * 1024))
    shared["ao"] = _tile_kn(np.asarray(inp["attn_w_o"][0], f32), 8)
    lamv = np.concatenate([np.asarray(inp[k][0], f32) for k in ("lambda_q1", "lambda_k1", "lambda_q2", "lambda_k2")])
    shared["lamv"] = np.ascontiguousarray(np.broadcast_to(lamv[None, :], (128, 512)))
    rel_bias = np.asarray(inp["rel_bias"], f32)
    maps = []
    for c in range(8):
        b, half = c // 2, c % 2
        m = dict(shared)
        xc = x[b, half * T:(half + 1) * T, :]
        m["x_in"] = np.ascontiguousarray(xc.T.reshape(NB, 128, T).transpose(1, 0, 2).reshape(128, NB * T))
        cst = np.zeros((128, NCONST), f32)
        ng = np.asarray(inp["norm_g"], f32)
        for l in range(DEPTH):
            for j in range(4):
                cst[:, C_G + (l * 4 + j) * 8: C_G + (l * 4 + j) * 8 + 8] = _pcol(ng[l, j])
        cwv = np.asarray(inp["conv_w"], f32)
        for ia in range(2):
            for tap in range(3):
                o = C_CW + (ia * 3 + tap) * 8
                cst[:, o:o + 8] = _pcol(cwv[ia, tap])
        cst[:, C_PS:C_PS + 8] = _pcol(np.asarray(inp["pool_scale"][0], f32))
        cst[:, C_SG:C_SG + 2] = _pcol(np.asarray(inp["attn_subln_g"][0], f32))
        cst[:, C_ML] = 1.0 if half == 1 else 0.0
        cst[:, C_MR] = 1.0 if half == 0 else 0.0
        for g, w in enumerate(POOLW):
            for i in range(8):
                for side, tl in ((0, i), (1, T - 8 + i)):
                    t = half * T + tl
                    lo = max(t - w // 2, 0)
                    hi = min(t + w - 1 - w // 2, SEQ - 1)
                    cst[:, (C_CL if side == 0 else C_CR) + g * 8 + i] = w / (hi - lo + 1)
        for h in range(4):
            cst[:, C_BC + h * 3 + 0] = rel_bias[15, h]
            cst[:, C_BC + h * 3 + 2] = rel_bias[31, h]
            cst[:, C_BC + h * 3 + 1] = rel_bias[31, h] if half == 0 else rel_bias[15, h]
        cst[:, C_EPS] = D * EPS
        cst[:, C_EPS + 1] = 256 * EPS
        m["consts"] = cst
        kl = np.arange(128)[:, None]
        y = np.arange(1152)[None, :]
        rel = kl - y + 4 * 128
        bk = _t5_bucket_np(rel)
        st = np.zeros((4, 128, 2 * 1152), f32)
        for h in range(4):
            near = rel_bias[bk, h]
            if half == 0:
                st[h, :, :1152] = near
                st[h, :, 1152:] = rel_bias[31, h]
            else:
                st[h, :, :1152] = rel_bias[15, h]
                st[h, :, 1152:] = near
        m["strips"] = st
        maps.append(m)
    return maps


_NC_CACHE = {}


def run_layers(layers, maps):
    key = tuple(layers)
    nc = build(layers)
    maps = [{k: m[k] for k in nc._used_inputs} for m in maps]
    res = run_bass_kernel_spmd(nc, maps, core_ids=list(range(8)))
    return [r["y_out"] for r in res.results]


def assemble(ys):
    out = np.zeros((4, SEQ, D), np.float32)
    for c in range(8):
        b, half = c // 2, c % 2
        yc = ys[c].reshape(128, NB, T).transpose(1, 0, 2).reshape(D, T).T
        out[b, half * T:(half + 1) * T, :] = yc
    return out


def kernel(**inputs):
    maps = prep_inputs(inputs)
    ys = run_layers(list(range(DEPTH)), maps)
    return assemble(ys)
```
